# Optimizing a Trainium2 kernel written in Bass

```python
import jax, jax.numpy as jnp
from jax import lax
import numpy as np

D_MODEL = 1024
BATCH = 2
SEQ = 8192
DEPTH = 4

HEAD_DIM = 64
D_MIX = D_MODEL
A_WIDTH = D_MIX // 4
A_HEADS = A_WIDTH // HEAD_DIM
B_WIDTH = D_MIX // 2
B_HEADS = B_WIDTH // HEAD_DIM
B_KV_HEADS = B_HEADS // 4
B_GROUP = B_HEADS // B_KV_HEADS
WINDOW = 128
C_WIDTH = D_MIX // 4
C_HEADS = C_WIDTH // HEAD_DIM
CHUNK = 64
D_FF = (11 * D_MODEL) // 4
CONV_WIDTH = 3
PLE_DIM = 256
ROPE_THETA = 10000.0
EPS = 1e-6

PART_SIZES = [A_WIDTH, A_WIDTH, A_WIDTH, A_WIDTH,
              B_WIDTH, B_KV_HEADS * HEAD_DIM, B_KV_HEADS * HEAD_DIM,
              C_WIDTH, C_WIDTH, C_WIDTH, C_WIDTH]
D_IN = int(sum(PART_SIZES))
SPLITS = [int(v) for v in np.cumsum(PART_SIZES)[:-1]]

kernel_name = "hybrid_hgrn2_swa_retention_trunk"


def _rmsnorm(x, gain=None):
    x32 = x.astype(jnp.float32)
    y = x32 * lax.rsqrt(jnp.mean(x32 * x32, axis=-1, keepdims=True) + EPS)
    if gain is not None:
        y = y * gain.astype(jnp.float32)
    return y


def _rope_tables(positions):
    inv = 1.0 / (ROPE_THETA ** (jnp.arange(0, HEAD_DIM, 2, dtype=jnp.float32) / HEAD_DIM))
    ang = positions.astype(jnp.float32)[..., None] * inv
    ang = jnp.concatenate([ang, ang], axis=-1)
    return jnp.cos(ang)[:, :, None, :], jnp.sin(ang)[:, :, None, :]


def _apply_rope(x, cos, sin):
    x = x.astype(jnp.float32)
    x1, x2 = jnp.split(x, 2, axis=-1)
    return x * cos + jnp.concatenate([-x2, x1], axis=-1) * sin


def _chunk(t):
    b, s, h, d = t.shape
    return t.reshape(b, s // CHUNK, CHUNK, h, d).transpose(0, 3, 1, 2, 4)


def _unchunk(t):
    b, h, n, c, d = t.shape
    return t.transpose(0, 2, 3, 1, 4).reshape(b, n * c, h, d)


def _hgrn2(q, f_logit, inp, g, lb, gnorm_gain):
    b, s, h, d = q.shape
    lb = lb.astype(jnp.float32).reshape(h, d)
    fl = f_logit.astype(jnp.float32)
    log_f = jnp.log(lb + (1.0 - lb) * jax.nn.sigmoid(fl))
    k = (1.0 - lb) * jax.nn.sigmoid(-fl)
    to_n = lambda t: jnp.moveaxis(_chunk(t.astype(jnp.float32)), 2, 0)
    qc, kc, vc, lfc = to_n(q), to_n(k), to_n(inp), to_n(log_f)
    mask3 = jnp.tril(jnp.ones((CHUNK, CHUNK), dtype=bool))[None, None, :, :, None]

    def step(state, xs):
        qn, kn, vn, lfn = xs
        bcum = jnp.cumsum(lfn, axis=2)
        diff = bcum[:, :, :, None, :] - bcum[:, :, None, :, :]
        dec = jnp.where(mask3, jnp.exp(jnp.where(mask3, diff, 0.0)), 0.0)
        scores = jnp.einsum('bhtk,bhsk,bhtsk->bhts', qn, kn, dec)
        o = jnp.einsum('bhts,bhsv->bhtv', scores, vn) \
            + jnp.einsum('bhtk,bhkv->bhtv', qn * jnp.exp(bcum), state)
        b_last = bcum[:, :, -1:, :]
        state = jnp.exp(b_last)[:, :, 0, :, None] * state \
            + jnp.einsum('bhsk,bhsv->bhkv', kn * jnp.exp(b_last - bcum), vn)
        return state, o

    s0 = jnp.zeros((b, h, d, inp.shape[-1]), jnp.float32)
    _, o = lax.scan(step, s0, (qc, kc, vc, lfc))
    o = _unchunk(jnp.moveaxis(o, 0, 2))
    o = _rmsnorm(o, gnorm_gain.reshape(h, -1)) * jax.nn.silu(g.astype(jnp.float32))
    return o.reshape(b, s, -1)


def _swa_sinks(q, k, v, sinks, cos, sin):
    b, s, hq, d = q.shape
    nb = s // WINDOW
    q = _apply_rope(q, cos, sin)
    k = _apply_rope(k, cos, sin)
    v = v.astype(jnp.float32)
    qb = q.reshape(b, nb, WINDOW, B_KV_HEADS, B_GROUP, d)

    def band(t):
        tp = jnp.pad(t, ((0, 0), (WINDOW, 0), (0, 0), (0, 0))).reshape(b, nb + 1, WINDOW, B_KV_HEADS, d)
        return jnp.concatenate([tp[:, :-1], tp[:, 1:]], axis=2)

    kb, vb = band(k), band(v)
    scores = jnp.einsum('bnqhgd,bnkhd->bnhgqk', qb, kb) * (d ** -0.5)
    qi = jnp.arange(WINDOW)[:, None]
    kj = jnp.arange(2 * WINDOW)[None, :]
    rel = qi + WINDOW - kj
    key_abs = jnp.arange(nb)[:, None, None] * WINDOW + kj[None] - WINDOW
    valid = ((rel >= 0) & (rel < WINDOW))[None] & (key_abs >= 0)
    scores = jnp.where(valid[None, :, None, None], scores, -jnp.inf)
    sink = jnp.broadcast_to(sinks.astype(jnp.float32).reshape(1, 1, B_KV_HEADS, B_GROUP, 1, 1),
                            scores.shape[:-1] + (1,))
    probs = jax.nn.softmax(jnp.concatenate([scores, sink], axis=-1), axis=-1)[..., :2 * WINDOW]
    o = jnp.einsum('bnhgqk,bnkhd->bnqhgd', probs, vb)
    return o.reshape(b, s, hq * d)


def _retention(q, k, v, g, cos, sin):
    b, s, h, d = q.shape
    q = _apply_rope(q, cos, sin)
    k = _apply_rope(k, cos, sin) * (d ** -0.5)
    qc, kc, vc = _chunk(q), _chunk(k), _chunk(v.astype(jnp.float32))
    log_g = jnp.log(1.0 - 2.0 ** (-5.0 - jnp.arange(h, dtype=jnp.float32)))
    idx = jnp.arange(CHUNK, dtype=jnp.float32)
    rel = idx[:, None] - idx[None, :]
    causal = rel >= 0
    dmat = jnp.where(causal[None], jnp.exp(jnp.where(causal[None], rel[None], 0.0) * log_g[:, None, None]), 0.0)
    intra = jnp.einsum('bhnid,bhnjd->bhnij', qc, kc) * dmat[None, :, None]
    o = jnp.einsum('bhnij,bhnjv->bhniv', intra, vc)
    k_dec = jnp.exp((CHUNK - 1.0 - idx)[None, :] * log_g[:, None])
    contrib = jnp.einsum('bhnjd,bhnjv->nbhdv', kc * k_dec[None, :, None, :, None], vc)
    chunk_decay = jnp.exp(CHUNK * log_g)[None, :, None, None]

    def step(state, u):
        return chunk_decay * state + u, state

    s0 = jnp.zeros((b, h, d, vc.shape[-1]), jnp.float32)
    _, s_before = lax.scan(step, s0, contrib)
    q_dec = jnp.exp((idx + 1.0)[None, :] * log_g[:, None])
    o = o + jnp.einsum('bhnid,nbhdv->bhniv', qc * q_dec[None, :, None, :, None], s_before)
    o = _unchunk(o)
    o = _rmsnorm(o) * jax.nn.silu(g.astype(jnp.float32))
    return o.reshape(b, s, -1)


def _conv_ffn(h, w_gate, w_up, conv_w, conv_b, w_down):
    gate = h @ w_gate
    gate = lax.conv_general_dilated(gate, conv_w[:, None, :].astype(gate.dtype), window_strides=(1,),
                                    padding=[(CONV_WIDTH - 1, 0)],
                                    dimension_numbers=('NWC', 'WIO', 'NWC'),
                                    feature_group_count=D_FF) + conv_b
    return (jax.nn.gelu(gate) * (h @ w_up)) @ w_down


def setup_inputs(seed: int = 0) -> dict:
    key = jax.random.key(seed)
    ks = jax.random.split(key, 20)
    nrm = lambda k, shape, scale: jax.random.normal(k, shape, jnp.float32) * scale
    x = nrm(ks[0], (BATCH, SEQ, D_MODEL), 1.0)
    p = nrm(ks[1], (DEPTH, BATCH, SEQ, PLE_DIM), 1.0)
    offset = jax.random.randint(ks[2], (BATCH, 1), 0, 4096, dtype=jnp.int32)
    positions = (offset + jnp.arange(SEQ, dtype=jnp.int32)[None, :]).astype(jnp.int32)
    return {
        "x": x,
        "p": p,
        "positions": positions,
        "attn_norm": 1.0 + nrm(ks[3], (DEPTH, D_MODEL), 0.02),
        "w_in": nrm(ks[4], (DEPTH, D_MODEL, D_IN), D_MODEL ** -0.5),
        "hgrn_lb": nrm(ks[5], (DEPTH, A_WIDTH), 0.5),
        "hgrn_gnorm": 1.0 + nrm(ks[6], (DEPTH, A_WIDTH), 0.02),
        "attn_sinks": nrm(ks[7], (DEPTH, B_HEADS), 1.0),
        "w_out": nrm(ks[8], (DEPTH, D_MIX, D_MODEL), 0.5 * D_MIX ** -0.5),
        "ffn_norm": 1.0 + nrm(ks[9], (DEPTH, D_MODEL), 0.02),
        "w_gate": nrm(ks[10], (DEPTH, D_MODEL, D_FF), D_MODEL ** -0.5),
        "w_up": nrm(ks[11], (DEPTH, D_MODEL, D_FF), D_MODEL ** -0.5),
        "conv_w": nrm(ks[12], (DEPTH, CONV_WIDTH, D_FF), CONV_WIDTH ** -0.5),
        "conv_b": nrm(ks[13], (DEPTH, D_FF), 0.02),
        "w_down": nrm(ks[14], (DEPTH, D_FF, D_MODEL), 0.5 * D_FF ** -0.5),
        "ple_norm": 1.0 + nrm(ks[15], (DEPTH, D_MODEL), 0.02),
        "w_ple_gate": nrm(ks[16], (DEPTH, D_MODEL, D_MODEL), D_MODEL ** -0.5),
        "w_ple_proj": nrm(ks[17], (DEPTH, PLE_DIM, D_MODEL), 0.5 * PLE_DIM ** -0.5),
        "final_norm": 1.0 + nrm(ks[18], (D_MODEL,), 0.02),
    }


def reference(x, p, positions, attn_norm, w_in, hgrn_lb, hgrn_gnorm, attn_sinks, w_out,
              ffn_norm, w_gate, w_up, conv_w, conv_b, w_down, ple_norm, w_ple_gate, w_ple_proj,
              final_norm):
    b, s, _ = x.shape
    dt = x.dtype
    cos, sin = _rope_tables(positions)
    lb_all = jnp.cumsum(jax.nn.softmax(hgrn_lb.astype(jnp.float32), axis=0), axis=0)
    lb_all = lb_all - lb_all[0:1]
    r = x
    for i in range(DEPTH):
        h = _rmsnorm(r, attn_norm[i]).astype(dt)
        z = h @ w_in[i]
        aq, af, ai, ag, bq, bk, bv, cq, ck, cv, cg = jnp.split(z, SPLITS, axis=-1)
        hd = lambda t: t.reshape(b, s, -1, HEAD_DIM)
        ya = _hgrn2(hd(aq), hd(af), hd(ai), hd(ag), lb_all[i], hgrn_gnorm[i])
        yb = _swa_sinks(hd(bq), hd(bk), hd(bv), attn_sinks[i], cos, sin)
        yc = _retention(hd(cq), hd(ck), hd(cv), hd(cg), cos, sin)
        mix = jnp.concatenate([ya, yb, yc], axis=-1).astype(dt)
        r = r + mix @ w_out[i]
        h2 = _rmsnorm(r, ffn_norm[i]).astype(dt)
        r = r + _conv_ffn(h2, w_gate[i], w_up[i], conv_w[i], conv_b[i], w_down[i])
        gate = jax.nn.sigmoid(_rmsnorm(r, ple_norm[i]).astype(dt) @ w_ple_gate[i])
        r = r + (p[i] @ w_ple_proj[i]) * gate
    return _rmsnorm(r, final_norm).astype(dt)
```

```python
from contextlib import ExitStack
import math
import numpy as np
import concourse.bass as bass
import concourse.mybir as mybir
from concourse.bass_utils import run_bass_kernel_spmd

F32 = mybir.dt.float32
BF = mybir.dt.bfloat16
I32 = mybir.dt.int32
AF = mybir.ActivationFunctionType
ALU = mybir.AluOpType
AX = mybir.AxisListType

NCORES = 8
T = 2048
NT = 16
NB = 4
D = 1024
DFF = 2816
NCH = 22
DEPTH = 4
NIN = 2816
EPS = 1e-6
NEG = -30000.0
XW = 520

C_ID = 0
C_CAUS = 128
C_MB = 192
C_SEG = 448
C_INV = 960
C_SIGN = 961
C_EAC = 962
C_EBC = 1090
C_LNC = 1218
C_EC = 1220
C_PM = 1224
NCONST = 1352


class Trk:
    __slots__ = ("w", "r")

    def __init__(self):
        self.w = None
        self.r = {}


class Ctx:
    NDMA = 24

    def __init__(self, nc, es):
        self.nc = nc
        self.eng = {"pe": nc.tensor, "dve": nc.vector, "act": nc.scalar, "pool": nc.gpsimd, "sp": nc.sync}
        self.sem = {}
        self.mult = {}
        for e in self.eng:
            self.sem[e] = es.enter_context(nc.semaphore("s_" + e))
            self.mult[e] = 1
        for i in range(self.NDMA):
            n = "d%d" % i
            self.sem[n] = es.enter_context(nc.semaphore("s_" + n))
            self.mult[n] = 16
        self.sem["cc"] = es.enter_context(nc.semaphore("s_cc"))
        self.mult["cc"] = 1
        self.cnt = {k: 0 for k in self.sem}
        self.waited = {e: {} for e in self.eng}
        self.dma_i = 0
        self.dma_pi = 0
        self.trk = {}

    def t(self, *key):
        k = tuple(key)
        v = self.trk.get(k)
        if v is None:
            v = Trk()
            self.trk[k] = v
        return v

    def _wait(self, e, toks):
        need = {}
        for (src, c) in toks:
            if need.get(src, 0) < c:
                need[src] = c
        w = self.waited[e]
        for src, c in need.items():
            if w.get(src, 0) < c:
                self.eng[e].wait_ge(self.sem[src], c * self.mult[src])
                w[src] = c

    def _deps(self, e, reads, writes):
        deps = []
        for t in reads:
            if t.w is not None and not (e == "pe" and t.w[0] == "pe"):
                deps.append(t.w)
        for t in writes:
            if t.w is not None and not (e == "pe" and t.w[0] == "pe"):
                deps.append(t.w)
            for src, c in t.r.items():
                if not (e == "pe" and src == "pe"):
                    deps.append((src, c))
        return deps

    def _record(self, tok, reads, writes):
        for t in reads:
            if t.r.get(tok[0], 0) < tok[1]:
                t.r[tok[0]] = tok[1]
        for t in writes:
            t.w = tok
            t.r = {}

    def op(self, e, fn, reads=(), writes=(), signal=True):
        self._wait(e, self._deps(e, reads, writes))
        ins = fn(self.eng[e])
        if signal:
            self.cnt[e] += 1
            ins.then_inc(self.sem[e], 1)
            tok = (e, self.cnt[e])
        else:
            tok = (e, self.cnt[e] + 1)
        self._record(tok, reads, writes)
        return tok

    def dma(self, e, out, in_, reads=(), writes=()):
        if e == "pool":
            i = self.dma_pi % 4
            self.dma_pi += 1
        else:
            i = 4 + self.dma_i % (self.NDMA - 4)
            self.dma_i += 1
        n = "d%d" % i
        deps = self._deps(e, reads, writes)
        if self.cnt[n] > 0:
            deps.append((n, self.cnt[n]))
        self._wait(e, deps)
        self.cnt[n] += 1
        self.eng[e].dma_start(out=out, in_=in_).then_inc(self.sem[n], 16)
        tok = (n, self.cnt[n])
        self._record(tok, reads, writes)
        return tok

    def collective(self, src_ap, dst_ap, groups, reads=(), writes=()):
        e = "pool"
        self._wait(e, self._deps(e, reads, writes))
        self.cnt["cc"] += 1
        self.nc.gpsimd.collective_compute("AllGather", ALU.bypass, replica_groups=groups,
                                          ins=[src_ap], outs=[dst_ap]).then_inc(self.sem["cc"])
        tok = ("cc", self.cnt["cc"])
        self._record(tok, reads, writes)
        return tok

    def barrier(self, final=False):
        waiters = ("dve", "act", "sp", "pool") + (("pe",) if final else ())
        toks_eng = [(o, self.cnt[o]) for o in ("pe", "dve", "act", "pool") if self.cnt[o] > 0]
        toks_dma = [(n, c) for n, c in self.cnt.items() if n.startswith("d") and n != "dve" and c > 0]
        for e in waiters:
            self._wait(e, [t for t in toks_eng if t[0] != e] + toks_dma)


def build_program(n_layers=DEPTH, stop=None):
    nc = bass.Bass("TRN2", target_bir_lowering=False)
    dt = nc.dram_tensor
    x_d = dt("x", [T, D], F32, kind="ExternalInput")
    p_d = dt("p", [DEPTH, T, 256], F32, kind="ExternalInput")
    pos_d = dt("pos", [1, T], I32, kind="ExternalInput")
    sel_d = dt("sel", [128, 8], F32, kind="ExternalInput")
    consts_d = dt("consts", [128, NCONST], F32, kind="ExternalInput")
    win_d = dt("w_in", [DEPTH, 128, 8, NIN], F32, kind="ExternalInput")
    wout_d = dt("w_out", [DEPTH, 128, 8, D], F32, kind="ExternalInput")
    wgu_d = dt("w_gu", [DEPTH, 128, 8, 2 * DFF], F32, kind="ExternalInput")
    wdn_d = dt("w_dn", [DEPTH, 128, NCH, D], F32, kind="ExternalInput")
    wpg_d = dt("w_pg", [DEPTH, 128, 8, D], F32, kind="ExternalInput")
    wpp_d = dt("w_pp", [DEPTH, 128, 2, D], F32, kind="ExternalInput")
    gains_d = dt("gains", [128, DEPTH * 3 * 8 + 8], F32, kind="ExternalInput")
    lb_d = dt("lb", [128, 2 * DEPTH], F32, kind="ExternalInput")
    gn_d = dt("gnorm", [1, DEPTH * 256], F32, kind="ExternalInput")
    sink_d = dt("sinks", [1, DEPTH * 8], F32, kind="ExternalInput")
    cw_d = dt("convw", [128, DEPTH * NCH * 3], F32, kind="ExternalInput")
    cb_d = dt("convb", [128, DEPTH * NCH], F32, kind="ExternalInput")
    out_d = dt("out", [T, D], F32, kind="ExternalOutput")
    dbg_d = dt("dbg", [128, 8, T], F32, kind="ExternalOutput") if stop is not None else None
    g_s = dt("g_s", [NT, 128, 512], BF)
    a_s = dt("a_s", [4, 128, T], BF)
    b_s = dt("b_s", [4, 128, T], BF)
    q_s = dt("q_s", [4, 128, T], BF)
    sp_s = dt("sp_s", [NT, 128, 512], BF)
    o_s = dt("o_s", [NT, 128, 512], F32)
    tab_s = dt("tab_s", [2, 128, T], F32)
    x1s = [dt("x1s%d" % l, [128, XW], F32) for l in range(DEPTH)]
    x1g = [dt("x1g%d" % l, [4 * 128, XW], F32) for l in range(DEPTH)]
    x2s = [dt("x2s%d" % l, [128, 16], F32) for l in range(DEPTH)]
    x2g = [dt("x2g%d" % l, [4 * 128, 16], F32) for l in range(DEPTH)]
    GROUPS = [[0, 1, 2, 3], [4, 5, 6, 7]]

    with ExitStack() as es:
        cx = Ctx(nc, es)
        uid = [0]

        def sb(name, shape, dtype, st=es):
            uid[0] += 1
            return st.enter_context(nc.sbuf_tensor("sb%d_%s" % (uid[0], name), shape, dtype))
        rT = sb("rT", [128, 8, T], F32)
        hmT = sb("hmT", [128, 8, T + 2], BF)
        ring = sb("ring", [128, 3, 4096], BF)
        cst = sb("cst", [128, NCONST], F32)
        identb = sb("identb", [128, 128], BF)
        permb = sb("permb", [128, 128], BF)
        maskb = sb("maskb", [128, 2, 128], BF)
        onesf = sb("onesf", [128, 128], F32)
        onesb = sb("onesb", [128, 128], BF)
        sqb = [sb("sqb%d" % i, [128, 2, 512], BF) for i in range(2)]
        gains = sb("gains", [128, DEPTH * 3 * 8 + 8], F32)
        lbraw = sb("lbraw", [128, 2, DEPTH], F32)
        lbv = sb("lbv", [128, 2, DEPTH], F32)
        oml = sb("oml", [128, 2, DEPTH], F32)
        esink = sb("esink", [128, DEPTH * 8], F32)
        convw = sb("convw", [128, DEPTH * NCH * 3], F32)
        convb = sb("convb", [128, DEPTH * NCH], F32)
        sel = sb("sel", [128, 8], F32)
        lnm = sb("lnm", [128, 4, 32], F32)
        lnd = sb("lnd", [128, 4, 32], F32)
        mtab = sb("mtab", [128, 4, 32], F32)
        dtab = sb("dtab", [128, 4, 32], F32)
        etab = sb("etab", [128, 4, 32], F32)
        cstab = sb("cstab", [128, 4, 32], F32)
        ctab = sb("ctab", [128, 4, 32], F32)
        dtot = sb("dtot", [128, 4], F32)
        tmp = [sb("tmp%d" % i, [128, 512], F32) for i in range(6)]
        banks = [es.enter_context(nc.psum_tensor("bank%d" % i, [128, 512], F32)) for i in range(8)]

        identf = cst[:, C_ID:C_ID + 128]
        caus = cst[:, C_CAUS:C_CAUS + 64]

        tmp_i = [0]

        def nexttmp():
            i = tmp_i[0] % len(tmp)
            tmp_i[0] += 1
            return tmp[i], cx.t("tmp", i)

        def bk(i):
            return banks[i], cx.t("bank", i)

        def mm_group(out_ap, pairs, reads, wtrk, extra_reads_last=()):
            n = len(pairs)
            for i, (l, r) in enumerate(pairs):
                last = i == n - 1
                cx.op("pe", lambda e, l=l, r=r, i=i, last=last: e.matmul(out_ap, lhsT=l, rhs=r, start=(i == 0), stop=last),
                      reads=reads if i == 0 else (), writes=[wtrk] if (i == 0 or last) else (), signal=last)
            tok = ("pe", cx.cnt["pe"])
            for t in reads:
                if t.r.get("pe", 0) < tok[1]:
                    t.r["pe"] = tok[1]

        def mm1(out_ap, l, r, reads, writes, start=True, stop=True, signal=True):
            cx.op("pe", lambda e: e.matmul(out_ap, lhsT=l, rhs=r, start=start, stop=stop), reads=reads, writes=writes,
                  signal=signal)

        def act(out, in_, func, reads, writes, scale=1.0, bias=0.0):
            cx.op("act", lambda e: e.activation(out=out, in_=in_, func=func, bias=bias, scale=scale), reads=reads, writes=writes)

        def tt(out, in0, in1, op, reads, writes, e="dve"):
            cx.op(e, lambda g: g.tensor_tensor(out=out, in0=in0, in1=in1, op=op), reads=reads, writes=writes)

        def ts(out, in0, s1, s2, op0, op1, reads, writes, e="dve"):
            if s2 is None:
                cx.op(e, lambda g: g.tensor_scalar(out=out, in0=in0, scalar1=s1, scalar2=None, op0=op0), reads=reads, writes=writes)
            else:
                cx.op(e, lambda g: g.tensor_scalar(out=out, in0=in0, scalar1=s1, scalar2=s2, op0=op0, op1=op1), reads=reads,
                      writes=writes)

        def stt(out, in0, scalar, in1, op0, op1, reads, writes):
            cx.op("dve", lambda g: g.scalar_tensor_tensor(out=out, in0=in0, scalar=scalar, in1=in1, op0=op0, op1=op1),
                  reads=reads, writes=writes)

        def cp(out, in_, reads, writes, e="dve"):
            if e == "act":
                cx.op("act", lambda g: g.copy(out=out, in_=in_), reads=reads, writes=writes)
            else:
                cx.op(e, lambda g: g.tensor_copy(out=out, in_=in_), reads=reads, writes=writes)

        def memset(ap, val, writes, e="dve"):
            cx.op(e, lambda g: g.memset(ap, val), writes=writes)

        def pipeline(stages, tiles, loads=None, lead=2, hook=None):
            n = len(stages)
            if loads is not None:
                for t_ in tiles[:lead]:
                    loads(t_)
            for i in range(len(tiles) + n - 1):
                if hook is not None and i == hook[0]:
                    hook[1]()
                for k in reversed(range(n)):
                    idx = i - k
                    if 0 <= idx < len(tiles):
                        stages[k](tiles[idx])
                if loads is not None and i + lead < len(tiles):
                    loads(tiles[i + lead])

        wunits = []

        def add_unit(ap3, nk, ncol):
            wunits.append((ap3, nk, ncol))

        for l in range(n_layers):
            for u in range(3):
                add_unit(win_d[l, :, :, u * 512:(u + 1) * 512], 8, 512)
            add_unit(win_d[l, :, :, 1536:1664], 8, 128)
            add_unit(win_d[l, :, :, 1664:2176], 8, 512)
            add_unit(win_d[l, :, :, 2176:2688], 8, 512)
            add_unit(win_d[l, :, :, 2688:2816], 8, 128)
            for u in range(2):
                add_unit(wout_d[l, :, :, u * 512:(u + 1) * 512], 8, 512)
            for hf in range(2):
                c0 = hf * 11
                for cpi in range(6):
                    c = c0 + 2 * cpi
                    ncc = 2 if cpi < 5 else 1
                    add_unit(wgu_d[l, :, :, c * 256:(c + ncc) * 256], 8, ncc * 256)
                for mq in range(4):
                    add_unit(wdn_d[l, :, c0:c0 + 11, mq * 256:(mq + 1) * 256], 11, 256)
            for u in range(2):
                add_unit(wpg_d[l, :, :, u * 512:(u + 1) * 512], 8, 512)
            add_unit(wpp_d[l, :, :, :], 2, 1024)
        wstate = {"issued": 0, "next": 0}

        def w_issue_upto(idx):
            while wstate["issued"] <= idx and wstate["issued"] < len(wunits):
                i = wstate["issued"]
                ap3, nk, ncol = wunits[i]
                slot = i % 3
                dst = ring[:, slot, 0:nk * ncol].rearrange("p (k c) -> p k c", c=ncol)
                cx.dma("pool", dst, ap3, writes=[cx.t("ring", slot)])
                wstate["issued"] += 1

        def w_next(nk, ncol, ahead=2):
            i = wstate["next"]
            assert wunits[i][1] == nk and wunits[i][2] == ncol, (i, wunits[i][1:], nk, ncol)
            w_issue_upto(i + ahead)
            wstate["next"] += 1
            slot = i % 3
            return ring[:, slot, 0:nk * ncol].rearrange("p (k c) -> p k c", c=ncol), cx.t("ring", slot)

        cx.dma("sp", cst[:], consts_d[:, :], writes=[cx.t("cst")])
        cx.dma("sp", gains[:], gains_d[:, :], writes=[cx.t("gains")])
        cx.dma("sp", lbraw[:].rearrange("p h l -> p (h l)"), lb_d[:, :], writes=[cx.t("lbraw")])
        cx.dma("sp", esink[:], bass.AP(sink_d, 0, [[0, 128], [1, DEPTH * 8]]), writes=[cx.t("esink")])
        cx.dma("sp", convw[:], cw_d[:, :], writes=[cx.t("convw")])
        cx.dma("sp", convb[:], cb_d[:, :], writes=[cx.t("convb")])
        cx.dma("sp", sel[:], sel_d[:, :], writes=[cx.t("sel")])
        w_issue_upto(1)
        cp(identb[:], identf, [cx.t("cst")], [cx.t("identb")])
        cp(permb[:], cst[:, C_PM:C_PM + 128], [cx.t("cst")], [cx.t("permb")])
        cp(maskb[:].rearrange("p a b -> p (a b)"), cst[:, C_MB:C_MB + 256], [cx.t("cst")], [cx.t("maskb")])
        memset(onesf[:], 1.0, [cx.t("onesf")])
        memset(onesb[:], 1.0, [cx.t("onesb")])
        act(esink[:], esink[:], AF.Exp, [cx.t("esink")], [cx.t("esink")])
        act(lbraw[:], lbraw[:], AF.Exp, [cx.t("lbraw")], [cx.t("lbraw")])
        ssum = tmp[0][:, 0:2]
        cx.op("dve", lambda g: g.reduce_sum(out=ssum, in_=lbraw[:], axis=AX.X), reads=[cx.t("lbraw")], writes=[cx.t("tmp", 0)])
        cx.op("dve", lambda g: g.reciprocal(out=tmp[0][:, 2:4], in_=ssum), reads=[cx.t("tmp", 0)], writes=[cx.t("tmp", 0)])
        tt(lbraw[:], lbraw[:], tmp[0][:, 2:4].unsqueeze(2).to_broadcast([128, 2, DEPTH]), ALU.mult,
           [cx.t("lbraw"), cx.t("tmp", 0)], [cx.t("lbraw")])
        memset(lbv[:, :, 0:1], 0.0, [cx.t("lbv")])
        for i in range(1, DEPTH):
            tt(lbv[:, :, i:i + 1], lbv[:, :, i - 1:i], lbraw[:, :, i:i + 1], ALU.add, [cx.t("lbraw"), cx.t("lbv")], [cx.t("lbv")])
        ts(oml[:], lbv[:], -1.0, 1.0, ALU.mult, ALU.add, [cx.t("lbv")], [cx.t("oml")])
        for hp in range(2):
            ts(lnm[:, 2 + hp, :], cst[:, C_SEG + 1:C_SEG + 33], cst[:, C_LNC + hp:C_LNC + hp + 1], None, ALU.mult, None,
               [cx.t("cst")], [cx.t("lnm")])
            ts(lnd[:, 2 + hp, :], cst[:, C_SEG + 1:C_SEG + 33], cst[:, C_LNC + hp:C_LNC + hp + 1], None, ALU.mult, None,
               [cx.t("cst")], [cx.t("lnd")])
            ts(mtab[:, 2 + hp, :], cst[:, C_SEG + 1:C_SEG + 33], cst[:, C_EC + hp:C_EC + hp + 1], None, ALU.mult, None,
               [cx.t("cst")], [cx.t("mtab")])
            ts(dtab[:, 2 + hp, :], cst[:, C_SEG + 1:C_SEG + 33], cst[:, C_EC + hp:C_EC + hp + 1], None, ALU.mult, None,
               [cx.t("cst")], [cx.t("dtab")])

        with ExitStack() as ps0:
            posi = sb("posi", [128, T], I32, ps0)
            th = sb("th", [128, T], F32, ps0)
            kf = sb("kf", [128, T], F32, ps0)
            rr = sb("rr", [128, T], F32, ps0)
            tpos = cx.t("postmp")
            C1 = 6.28125
            C2 = 2.0 * math.pi - 6.28125
            tsteps = []
            tsteps.append(lambda: cx.dma("sp", posi[:], bass.AP(pos_d, 0, [[0, 128], [1, T]]), writes=[tpos]))
            tsteps.append(lambda: cp(th[:], posi[:], [tpos], [tpos]))
            tsteps.append(lambda: ts(th[:], th[:], cst[:, C_INV:C_INV + 1], None, ALU.mult, None, [tpos, cx.t("cst")], [tpos]))
            tsteps.append(lambda: ts(posi[:], th[:], 1.0 / (2.0 * math.pi), None, ALU.mult, None, [tpos], [tpos]))
            tsteps.append(lambda: cp(kf[:], posi[:], [tpos], [tpos]))
            tsteps.append(lambda: stt(rr[:], kf[:], -C1, th[:], ALU.mult, ALU.add, [tpos], [tpos]))
            tsteps.append(lambda: stt(rr[:], kf[:], -C2, rr[:], ALU.mult, ALU.add, [tpos], [tpos]))
            tsteps.append(lambda: ts(kf[:], rr[:], math.pi, None, ALU.is_gt, None, [tpos], [tpos]))
            tsteps.append(lambda: stt(rr[:], kf[:], -2.0 * math.pi, rr[:], ALU.mult, ALU.add, [tpos], [tpos]))
            tsteps.append(lambda: ts(kf[:], rr[:], -math.pi, None, ALU.is_lt, None, [tpos], [tpos]))
            tsteps.append(lambda: stt(rr[:], kf[:], 2.0 * math.pi, rr[:], ALU.mult, ALU.add, [tpos], [tpos]))
            tsteps.append(lambda: act(th[:], rr[:], AF.Sin, [tpos, cx.t("cst")], [tpos], scale=cst[:, C_SIGN:C_SIGN + 1]))
            tsteps.append(lambda: cx.dma("act", tab_s[1, :, :], th[:], reads=[tpos], writes=[cx.t("tab_s")]))
            tsteps.append(lambda: ts(rr[:], rr[:], math.pi / 2.0, None, ALU.add, None, [tpos], [tpos]))
            tsteps.append(lambda: ts(kf[:], rr[:], math.pi, None, ALU.is_gt, None, [tpos], [tpos]))
            tsteps.append(lambda: stt(rr[:], kf[:], -2.0 * math.pi, rr[:], ALU.mult, ALU.add, [tpos], [tpos]))
            tsteps.append(lambda: act(kf[:], rr[:], AF.Sin, [tpos], [tpos]))
            tsteps.append(lambda: cx.dma("act", tab_s[0, :, :], kf[:], reads=[tpos], writes=[cx.t("tab_s")]))
            xb = [sb("xb%d" % i, [128, D], F32, ps0) for i in range(2)]
            for j in range(NT):
                s = j % 2
                cx.dma("sp", xb[s][:], x_d[j * 128:(j + 1) * 128, :], writes=[cx.t("xb", s)])
                for half in range(2):
                    b, bt = bk(half + 2 * (j % 2))
                    for q in range(4):
                        m = half * 4 + q
                        mm1(b[:, q * 128:(q + 1) * 128], xb[s][:, m * 128:(m + 1) * 128], identf, [cx.t("xb", s), cx.t("cst")],
                            [bt], signal=(q == 3))
                    cp(rT[:, half * 4:half * 4 + 4, j * 128:(j + 1) * 128], b[:].rearrange("p (q t) -> p q t", t=128), [bt],
                       [cx.t("rT", half * 4 + q, j // 4) for q in range(4)], e=("act" if half else "dve"))
                for _ in range(2):
                    if tsteps:
                        tsteps.pop(0)()
            while tsteps:
                tsteps.pop(0)()
            cx.barrier()

        def rT_trks(tb):
            return [cx.t("rT", k, tb) for k in range(8)]

        def hm_trks(tb):
            return [cx.t("hm", k, tb) for k in range(8)]

        epsc = sb("epsc", [128, 1], F32)
        sq2 = sb("sq2", [128, 8, 2], BF)
        rs2 = sb("rs2", [128, 2], F32)
        xs2 = sb("xs2", [128, 16], F32)
        memset(epsc[:], EPS, [cx.t("epsc")])

        sq_i = [0]

        def rstd_block2(tb, bank_i):
            b, bt = bk(bank_i)
            for kk in range(4):
                si = sq_i[0] % 2
                sq_i[0] += 1
                act(sqb[si][:], rT[:, 2 * kk:2 * kk + 2, tb * 512:(tb + 1) * 512], AF.Square,
                    [cx.t("rT", 2 * kk, tb), cx.t("rT", 2 * kk + 1, tb)], [cx.t("sqb", si)])
                for i2 in range(2):
                    k = 2 * kk + i2
                    mm1(b[:], onesb[:], sqb[si][:, i2, :], [cx.t("sqb", si), cx.t("onesb")], [bt] if (k == 0 or k == 7) else (),
                        start=(k == 0), stop=(k == 7), signal=(i2 == 1))
            rs, rst = nexttmp()
            act(rs[:], b[:], AF.Ln, [bt, cx.t("epsc")], [rst], scale=1.0 / D, bias=epsc[:, 0:1])
            act(rs[:], rs[:], AF.Exp, [rst], [rst], scale=-0.5)
            return rs, rst

        def norm_to_hm(goff):
            for tb in range(NB):
                rs, rst = rstd_block2(tb, tb % 2)
                for k in range(8):
                    stt(hmT[:, k, 2 + tb * 512:2 + (tb + 1) * 512], rT[:, k, tb * 512:(tb + 1) * 512], gains[:, goff + k:goff + k + 1],
                        rs[:], ALU.mult, ALU.mult, [cx.t("rT", k, tb), rst, cx.t("gains")], [cx.t("hm", k, tb)])

        def dump(stage_l, stage_s):
            if stop is not None and stop == (stage_l, stage_s):
                cx.barrier()
                cx.dma("sp", dbg_d[:, :, :], rT[:], reads=[cx.t("rT", k, tb) for k in range(8) for tb in range(NB)],
                       writes=[cx.t("dbg")])
                return True
            return False

        stopped = False
        for l in range(n_layers):
            if stopped:
                break
            G_ATTN = (l * 3 + 0) * 8
            G_FFN = (l * 3 + 1) * 8
            G_PLE = (l * 3 + 2) * 8
            with ExitStack() as ph:
                kTa = sb("kTa", [128, 128 + T], BF, ph)
                vaug = sb("vaug", [128, NT + 1, 2, 65], BF, ph)
                gnl = sb("gnl", [128, 256], F32, ph)
                cx.dma("sp", gnl[:], bass.AP(gn_d, l * 256, [[0, 128], [1, 256]]), writes=[cx.t("gnl")])
                memset(vaug[:, 1:, :, 64:65], 1.0, [cx.t("vaug_ones")])
                norm_to_hm(G_ATTN)

                with ExitStack() as ph1:
                    cosT = sb("cosT", [128, T], F32, ph1)
                    ssT = sb("ssT", [128, T], F32, ph1)
                    blkb = [sb("blkb%d" % i, [128, 512], BF, ph1) for i in range(6)]
                    bi = [0]

                    def nextblk():
                        i = bi[0] % 6
                        bi[0] += 1
                        return blkb[i], cx.t("blkb", i)

                    cx.dma("sp", cosT[:], tab_s[0, :, :], reads=[cx.t("tab_s")], writes=[cx.t("cosT")])
                    cx.dma("sp", ssT[:], tab_s[1, :, :], reads=[cx.t("tab_s")], writes=[cx.t("ssT")])
                    w, wt = w_next(8, 512)
                    for tb in range(NB):
                        bs = [bk(4 * (tb % 2) + q) for q in range(4)]
                        hsl = slice(2 + tb * 512, 2 + (tb + 1) * 512)
                        blk = slice(tb * 512, (tb + 1) * 512)
                        for q in range(4):
                            mm_group(bs[q][0][:], [(w[:, k, q * 128:(q + 1) * 128], hmT[:, k, hsl]) for k in range(8)],
                                     hm_trks(tb) + [wt], bs[q][1])
                        for hp in range(2):
                            (qb, qbt), (fb, fbt) = bs[2 * hp], bs[2 * hp + 1]
                            sig, sigt = nexttmp()
                            km, kmt = nexttmp()
                            lf, lft = nexttmp()
                            act(sig[:], fb[:], AF.Sigmoid, [fbt], [sigt])
                            ts(km[:], sig[:], -1.0, 1.0, ALU.mult, ALU.add, [sigt], [kmt], e="pool")
                            act(lf[:], sig[:], AF.Ln, [sigt, cx.t("oml"), cx.t("lbv")], [lft], scale=oml[:, hp, l:l + 1],
                                bias=lbv[:, hp, l:l + 1])
                            B, Bt = sig, sigt
                            cx.op("dve", lambda g, B=B, lf=lf: g.tensor_tensor_scan(out=B[:], data0=cst[:, C_SEG:C_SEG + 512], data1=lf[:],
                                                                                  initial=0.0, op0=ALU.mult, op1=ALU.add),
                                  reads=[lft, cx.t("cst")], writes=[Bt])
                            Bv = B[:].rearrange("p (n k) -> p n k", k=64)
                            cp(lnm[:, hp, tb * 8:(tb + 1) * 8], Bv[:, :, 31], [Bt], [cx.t("lnm")])
                            act(mtab[:, hp, tb * 8:(tb + 1) * 8], Bv[:, :, 31], AF.Exp, [Bt], [cx.t("mtab")])
                            Dt_, Dtt = lf, lft
                            tt(Dt_[:].rearrange("p (n k) -> p n k", k=64), Bv, Bv[:, :, 31:32].to_broadcast([128, 8, 64]),
                               ALU.subtract, [Bt], [Dtt])
                            Dv = Dt_[:].rearrange("p (n k) -> p n k", k=64)
                            cp(lnd[:, hp, tb * 8:(tb + 1) * 8], Dv[:, :, 63], [Dtt], [cx.t("lnd")])
                            EA, EAt = nexttmp()
                            EB, EBt = sig, sigt
                            act(EA[:], Dt_[:], AF.Exp, [Dtt], [EAt])
                            act(EB[:], Dt_[:], AF.Exp, [Dtt], [EBt], scale=-1.0)
                            cp(dtab[:, hp, tb * 8:(tb + 1) * 8], EA[:].rearrange("p (n k) -> p n k", k=64)[:, :, 63], [EAt],
                               [cx.t("dtab")])
                            ab, abt = nextblk()
                            tt(ab[:], qb[:], EA[:], ALU.mult, [qbt, EAt], [abt])
                            cx.dma("sp", a_s[hp, :, blk], ab[:], reads=[abt], writes=[cx.t("a_s", tb)])
                            bb2, bbt2 = nextblk()
                            stt(bb2[:], km[:], oml[:, hp, l:l + 1], EB[:], ALU.mult, ALU.mult, [kmt, EBt, cx.t("oml")], [bbt2])
                            cx.dma("sp", b_s[hp, :, blk], bb2[:], reads=[bbt2], writes=[cx.t("b_s", tb)])

                    xsb = [sb("xsb%d" % i, [128, 512], BF, ph1) for i in range(4)]
                    hcount = [0]

                    def rope_unit(w, wt, nch, finish):
                        halves = []
                        for tb in range(NB):
                            for h0 in range(0, nch, 2):
                                halves.append((tb, list(range(h0, min(h0 + 2, nch)))))
                        state = {}

                        def G(hi):
                            tb, cis = halves[hi]
                            st = hcount[0] % 2
                            hcount[0] += 1
                            state[hi] = st
                            hsl = slice(2 + tb * 512, 2 + (tb + 1) * 512)
                            for n_, ci in enumerate(cis):
                                xb_, xbt = bk(4 * st + n_)
                                mm_group(xb_[:], [(w[:, k, ci * 128:(ci + 1) * 128], hmT[:, k, hsl]) for k in range(8)], hm_trks(tb) + [wt], xbt)
                                cp(xsb[2 * st + n_][:], xb_[:], [xbt], [cx.t("xsb", 2 * st + n_)], e="act")

                        def Pm(hi):
                            tb, cis = halves[hi]
                            st = state[hi]
                            blk = slice(tb * 512, (tb + 1) * 512)
                            for n_, ci in enumerate(cis):
                                xb_, xbt = bk(4 * st + n_)
                                xs_, xst = bk(4 * st + 2 + n_)
                                mm1(xs_[:], permb[:], xsb[2 * st + n_][:], [cx.t("permb"), cx.t("xsb", 2 * st + n_)], [xst])
                                t1, t1t = nexttmp()
                                t2, t2t = nexttmp()
                                tt(t1[:], xb_[:], cosT[:, blk], ALU.mult, [xbt, cx.t("cosT")], [t1t])
                                tt(t2[:], xs_[:], ssT[:, blk], ALU.mult, [xst, cx.t("ssT")], [t2t])
                                finish(ci, tb, t1, t1t, t2, t2t)

                        for hi in range(len(halves) + 1):
                            if hi < len(halves):
                                G(hi)
                            if hi >= 1:
                                Pm(hi - 1)

                    def fin_c(ci, tb, t1, t1t, t2, t2t):
                        hp, which = ci // 2, ci % 2
                        blk = slice(tb * 512, (tb + 1) * 512)
                        tt(t1[:], t1[:], t2[:], ALU.add, [t1t, t2t], [t1t], e="pool")
                        cbase = (C_EBC if which else C_EAC) + hp * 64
                        ctab = cst[:, cbase:cbase + 64]
                        ob_, obt_ = nextblk()
                        tt(ob_[:].rearrange("p (n k) -> p n k", k=64), t1[:].rearrange("p (n k) -> p n k", k=64),
                           ctab.unsqueeze(1).to_broadcast([128, 8, 64]), ALU.mult, [t1t, cx.t("cst")], [obt_], e="pool")
                        dst = (b_s if which else a_s)
                        cx.dma("sp", dst[2 + hp, :, blk], ob_[:], reads=[obt_], writes=[cx.t("b_s" if which else "a_s", tb)])

                    def fin_q(ci, tb, t1, t1t, t2, t2t):
                        blk = slice(tb * 512, (tb + 1) * 512)
                        ob_, obt_ = nextblk()
                        tt(ob_[:], t1[:], t2[:], ALU.add, [t1t, t2t], [obt_], e="pool")
                        cx.dma("sp", q_s[ci, :, blk], ob_[:], reads=[obt_], writes=[cx.t("q_s", tb)])

                    def fin_k(ci, tb, t1, t1t, t2, t2t):
                        tt(kTa[:, 128 + tb * 512:128 + (tb + 1) * 512], t1[:], t2[:], ALU.add, [t1t, t2t], [cx.t("kTa", tb)])

                    w, wt = w_next(8, 512)
                    rope_unit(w, wt, 4, fin_c)
                    w, wt = w_next(8, 512)
                    rope_unit(w, wt, 4, fin_q)
                    w, wt = w_next(8, 128)
                    rope_unit(w, wt, 1, fin_k)
                    cx.barrier()

                wv, wvt = w_next(8, 512, 2)
                wg, wgt = w_next(8, 512, 1)
                wbv, wbvt = w_next(8, 128, 0)
                with ExitStack() as ph2:
                    v_all = sb("v_all", [128, NT, 512], BF, ph2)
                    a_t = [sb("a_t%d" % i, [128, 4, 128], BF, ph2) for i in range(6)]
                    bt_t = [sb("bt_t%d" % i, [128, 4, 128], BF, ph2) for i in range(6)]
                    b_t = [sb("b_t%d" % i, [128, 512], BF, ph2) for i in range(2)]
                    g_t = [sb("g_t%d" % i, [128, 512], BF, ph2) for i in range(2)]
                    o_t = [sb("o_t%d" % i, [128, 512], F32, ph2) for i in range(2)]
                    scm = [sb("scm%d" % i, [128, 4, 2, 64], BF, ph2) for i in range(2)]
                    sp_t = [sb("sp_t%d" % i, [128, 2, 4, 64], BF, ph2) for i in range(2)]
                    Spf = sb("Spf", [128, 4, 64], F32, ph2)
                    U = sb("U", [128, 4, 64], F32, ph2)
                    Sfull = sb("Sfull", [128, 4, 64], F32, ph2)
                    xs = sb("xs", [128, XW], F32, ph2)
                    memset(Spf[:], 0.0, [cx.t("Spf")])
                    for j in range(NT):
                        s = j % 2
                        tb = j // 4
                        hcol = slice(2 + j * 128, 2 + (j + 1) * 128)
                        bv_, bvt = bk(3 * (j % 2))
                        mm_group(bv_[:], [(hmT[:, k, hcol], wv[:, k, :]) for k in range(8)], hm_trks(tb) + [wvt], bvt)
                        cp(v_all[:, j, :], bv_[:], [bvt], [cx.t("v_all", j)], e="dve")
                        bg_, bgt = bk(3 * (j % 2) + 1)
                        mm_group(bg_[:], [(hmT[:, k, hcol], wg[:, k, :]) for k in range(8)], hm_trks(tb) + [wgt], bgt)
                        act(g_t[s][:], bg_[:], AF.Silu, [bgt], [cx.t("g_t", s)])
                        tt(g_t[s][:, 0:256], g_t[s][:, 0:256], gnl[:], ALU.mult, [cx.t("g_t", s), cx.t("gnl")], [cx.t("g_t", s)], e="pool")
                        cx.dma("sp", g_s[j, :, :], g_t[s][:], reads=[cx.t("g_t", s)], writes=[cx.t("g_s", j)])
                        bw_, bwt = bk(3 * (j % 2) + 2)
                        mm_group(bw_[:, 0:128], [(hmT[:, k, hcol], wbv[:, k, :]) for k in range(8)], hm_trks(tb) + [wbvt], bwt)
                        cp(vaug[:, 1 + j, :, 0:64], bw_[:, 0:128].rearrange("p (a b) -> p a b", b=64), [bwt], [cx.t("vaug", 1 + j)], e="act")


                    def load_ab(j):
                        s_ = j % 6
                        tb_ = j // 4
                        cx.dma("sp", a_t[s_][:], a_s[:, :, j * 128:(j + 1) * 128].rearrange("h p t -> p h t"), reads=[cx.t("a_s", tb_)],
                               writes=[cx.t("a_t", s_)])
                        cx.dma("sp", bt_t[s_][:], b_s[:, :, j * 128:(j + 1) * 128].rearrange("h p t -> p h t"), reads=[cx.t("b_s", tb_)],
                               writes=[cx.t("bt_t", s_)])

                    tt(ctab[:, :, 0:31], dtab[:, :, 0:31], mtab[:, :, 1:32], ALU.mult, [cx.t("dtab"), cx.t("mtab")], [cx.t("ctab")])
                    tt(cstab[:], lnm[:], lnd[:], ALU.add, [cx.t("lnm"), cx.t("lnd")], [cx.t("cstab")])
                    for hp in range(4):
                        cx.op("dve", lambda g, hp=hp: g.tensor_tensor_scan(out=etab[:, hp, :], data0=cst[:, C_SEG + 1:C_SEG + 33],
                                                                        data1=cstab[:, hp, :], initial=0.0, op0=ALU.mult, op1=ALU.add),
                              reads=[cx.t("cstab"), cx.t("cst")], writes=[cx.t("etab")])
                    act(dtot[:], etab[:, :, 31], AF.Exp, [cx.t("etab")], [cx.t("dtot")])
                    tt(cstab[:], etab[:], lnd[:], ALU.subtract, [cx.t("etab"), cx.t("lnd")], [cx.t("cstab")])
                    act(etab[:], cstab[:], AF.Exp, [cx.t("cstab")], [cx.t("etab")])

                    def f0(j):
                        s = j % 2
                        s4 = j % 6
                        bb_, bbt = bk(s)
                        for hp in range(4):
                            mm1(bb_[:, hp * 128:(hp + 1) * 128], bt_t[s4][:, hp, :], identb[:], [cx.t("bt_t", s4), cx.t("identb")],
                                [bbt], signal=(hp == 3))

                    def f1(j):
                        s = j % 2
                        bb_, bbt = bk(s)
                        cp(b_t[s][:], bb_[:], [bbt], [cx.t("b_t", s)], e="act")

                    def f2(j):
                        s = j % 2
                        s4 = j % 6
                        Xb = [bk(2 + 2 * s), bk(3 + 2 * s)]
                        for e2 in range(2):
                            er = slice(e2 * 64, (e2 + 1) * 64)
                            for ch in range(2):
                                cc = slice(ch * 64, (ch + 1) * 64)
                                for hp in range(4):
                                    mm1(Xb[e2][0][ch * 64:(ch + 1) * 64, hp * 64:(hp + 1) * 64], bt_t[s4][er, hp, cc], a_t[s4][er, hp, cc],
                                        [cx.t("bt_t", s4), cx.t("a_t", s4)], [Xb[e2][1]], signal=(ch == 1 and hp == 3))
                        for ch in range(2):
                            rows = slice(ch * 64, (ch + 1) * 64)
                            for h8 in range(8):
                                hp, e2 = h8 // 2, h8 % 2
                                mm1(Xb[ch][0][e2 * 64:(e2 + 1) * 64, 256 + hp * 64:256 + (hp + 1) * 64], b_t[s][rows, h8 * 64:(h8 + 1) * 64],
                                    v_all[rows, j, h8 * 64:(h8 + 1) * 64], [cx.t("b_t", s), cx.t("v_all", j)], [Xb[ch][1]],
                                    signal=(h8 == 7))

                    def f3(j):
                        s = j % 2
                        Xb = [bk(2 + 2 * s), bk(3 + 2 * s)]
                        for e2 in range(2):
                            tt(scm[s][:, :, e2, :], Xb[e2][0][:, 0:256].rearrange("p (h t) -> p h t", t=64),
                               caus.unsqueeze(1).to_broadcast([128, 4, 64]), ALU.mult, [Xb[e2][1], cx.t("cst")], [cx.t("scm", s, e2)])
                        cp(sp_t[s][:, 0, :, :], Spf[:], [cx.t("Spf")], [cx.t("sp_t", s)], e="act")
                        for ch in range(2):
                            n = 2 * j + ch
                            tt(U[:], Spf[:], Xb[ch][0][:, 256:512].rearrange("p (h c) -> p h c", c=64), ALU.add, [cx.t("Spf"), Xb[ch][1]],
                               [cx.t("U")])
                            if n < 31:
                                tt(Spf[:], U[:], ctab[:, :, n:n + 1].to_broadcast([128, 4, 64]), ALU.mult, [cx.t("U"), cx.t("ctab")],
                                   [cx.t("Spf")])
                                if ch == 0:
                                    cp(sp_t[s][:, 1, :, :], Spf[:], [cx.t("Spf")], [cx.t("sp_t", s)], e="act")
                            else:
                                tt(Sfull[:], U[:], dtab[:, :, n:n + 1].to_broadcast([128, 4, 64]), ALU.mult, [cx.t("U"), cx.t("dtab")],
                                   [cx.t("Sfull")])
                        cx.dma("sp", sp_s[j, :, :], sp_t[s][:].rearrange("p a h c -> p (a h c)"), reads=[cx.t("sp_t", s)], writes=[cx.t("sp_s", j)])

                    def f4(j):
                        s = j % 2
                        Rb = [bk(6), bk(7)]
                        for ch in range(2):
                            rows = slice(ch * 64, (ch + 1) * 64)
                            for h8 in range(8):
                                hp, e2 = h8 // 2, h8 % 2
                                mm1(Rb[ch][0][rows, h8 * 64:(h8 + 1) * 64], scm[s][rows, hp, e2, :], v_all[rows, j, h8 * 64:(h8 + 1) * 64],
                                    [cx.t("scm", s, e2), cx.t("v_all", j)], [Rb[ch][1]], signal=(h8 == 7))

                    def f5(j):
                        s = j % 2
                        Rb = [bk(6), bk(7)]
                        cp(o_t[s][0:64, :], Rb[0][0][0:64, :], [Rb[0][1]], [cx.t("o_t", s)], e="act")
                        cp(o_t[s][64:128, :], Rb[1][0][64:128, :], [Rb[1][1]], [cx.t("o_t", s)], e="act")
                        cx.dma("sp", o_s[j, :, :], o_t[s][:], reads=[cx.t("o_t", s)], writes=[cx.t("o_s", j)])

                    def f_half(j):
                        pass

                    pipeline([f0, f1, f2, f3, f4, f5, f_half], list(range(NT)), loads=load_ab, lead=4)

                    cp(xs[:, 0:256], Sfull[:].rearrange("p h c -> p (h c)"), [cx.t("Sfull")], [cx.t("xs")])
                    cp(xs[:, 256:260], dtot[:], [cx.t("dtot")], [cx.t("xs")])
                    cp(xs[:, 260:388], kTa[:, 128 + 15 * 128:128 + 16 * 128], [cx.t("kTa", 3)], [cx.t("xs")])
                    cp(xs[:, 388:518], vaug[:, NT, :, :].rearrange("p a b -> p (a b)"), [cx.t("vaug", NT), cx.t("vaug_ones")], [cx.t("xs")])
                    memset(xs[:, 518:520], 0.0, [cx.t("xs")])
                    cx.dma("sp", x1s[l][:, :], xs[:], reads=[cx.t("xs")], writes=[cx.t("x1s", l)])
                    cx.collective(x1s[l].ap().opt(), x1g[l].ap().opt(), GROUPS, reads=[cx.t("x1s", l)], writes=[cx.t("x1g", l)])
                    cx.barrier()

                with ExitStack() as ph3:
                    S_in = sb("S_in", [128, 4, 64], F32, ph3)
                    with ExitStack() as ph3a:
                        kp = [sb("kp%d" % i, [128, 2, 2, 128], BF, ph3a) for i in range(2)]
                        q_t = [sb("q_t%d" % i, [128, 4, 128], BF, ph3a) for i in range(7)]
                        xg = sb("xg", [128, 4, XW], F32, ph3a)
                        X = sb("X", [128, 4, 64], F32, ph3a)
                        PT = [sb("PT%d" % i, [128, 512], BF, ph3a) for i in range(8)]
                        den = sb("den", [128, 8], F32, ph3a)
                        osw = sb("osw", [128, 8, 64], BF, ph3a)
                        memset(kp[0][:], 0.0, [cx.t("kp", 0)])
                        memset(kp[1][:], 0.0, [cx.t("kp", 1)])

                        def load_q(j, slot):
                            cx.dma("sp", q_t[slot][:], q_s[:, :, j * 128:(j + 1) * 128].rearrange("c p t -> p c t"), reads=[cx.t("q_s", j // 4)],
                                   writes=[cx.t("q_t", slot)])

                        def swa_kp(j, slot):
                            tb = j // 4
                            pp = j % 2
                            kpt = cx.t("kp", pp)
                            ktr = [cx.t("kTa", tb)] + ([cx.t("kTa", tb - 1)] if (j % 4 == 0 and j > 0) else []) + ([cx.t("kTa_halo")] if j == 0 else [])
                            for kv in range(2):
                                r = slice(kv * 64, (kv + 1) * 64)
                                cp(kp[pp][r, kv, :, :].rearrange("p a b -> p (a b)"), kTa[r, j * 128:j * 128 + 256], ktr, [kpt], e="dve")

                        def swa_qk(j, slot):
                            pp = j % 2
                            kpt = cx.t("kp", pp)
                            SW = [bk(1), bk(2), bk(3), bk(4)]
                            qtr = [cx.t("q_t", slot)]
                            for kv in range(2):
                                for kb in range(2):
                                    b_, bt_ = SW[kv * 2 + kb]
                                    mm1(b_[:], kp[pp][:, kv, kb, :], q_t[slot][:], [kpt] + qtr, [bt_], start=True, stop=False, signal=False)
                                    mm1(b_[:], identb[:], maskb[:, kb, :].unsqueeze(1).to_broadcast([128, 4, 128]),
                                        [cx.t("identb"), cx.t("maskb")], [bt_], start=False, stop=True)

                        def swa_exp(j, slot):
                            pp = j % 2
                            SW = [bk(1), bk(2), bk(3), bk(4)]
                            for kv in range(2):
                                for kb in range(2):
                                    b_, bt_ = SW[kv * 2 + kb]
                                    act(PT[pp * 4 + kv * 2 + kb][:], b_[:], AF.Exp, [bt_], [cx.t("PT", pp * 4 + kv * 2 + kb)], scale=0.125)

                        def swa_mask(j):
                            pp = j % 2
                            for kv in range(2):
                                for kb in range(2):
                                    pi = pp * 4 + kv * 2 + kb
                                    tt(PT[pi][:].rearrange("p (c q) -> p c q", q=128), PT[pi][:].rearrange("p (c q) -> p c q", q=128),
                                       maskb[:, kb, :].unsqueeze(1).to_broadcast([128, 4, 128]), ALU.mult, [cx.t("PT", pi), cx.t("maskb")],
                                       [cx.t("PT", pi)])

                        def swa_pv(j):
                            pp = j % 2
                            OB = [bk(5), bk(6)]
                            vtr = [cx.t("vaug", j), cx.t("vaug", j + 1), cx.t("vaug_ones")]
                            for h in range(8):
                                kv, c = h // 4, h % 4
                                ob, obt = OB[h // 4]
                                oo = ob[:, (h % 4) * 65:(h % 4) * 65 + 65]
                                p0, p1 = pp * 4 + kv * 2 + 0, pp * 4 + kv * 2 + 1
                                mm1(oo, PT[p0][:, c * 128:(c + 1) * 128], vaug[:, j, kv, :], [cx.t("PT", p0)] + vtr, [obt],
                                    start=True, stop=False, signal=False)
                                mm1(oo, PT[p1][:, c * 128:(c + 1) * 128], vaug[:, j + 1, kv, :], [cx.t("PT", p1)] + vtr, [obt],
                                    start=False, stop=True, signal=(h % 4 == 3))

                        def swa_norm(j):
                            OB = [bk(5), bk(6)]
                            for q in range(2):
                                obv = OB[q][0][:, 0:260].rearrange("p (h c) -> p h c", c=65)
                                tt(den[:, q * 4:(q + 1) * 4], obv[:, :, 64], esink[:, l * 8 + q * 4:l * 8 + (q + 1) * 4], ALU.add,
                                   [OB[q][1], cx.t("esink")], [cx.t("den")])
                            cx.op("dve", lambda g: g.reciprocal(out=den[:], in_=den[:]), reads=[cx.t("den")], writes=[cx.t("den")])
                            for q in range(2):
                                obv = OB[q][0][:, 0:260].rearrange("p (h c) -> p h c", c=65)
                                tt(osw[:, q * 4:(q + 1) * 4, :], obv[:, :, 0:64], den[:, q * 4:(q + 1) * 4].unsqueeze(2).to_broadcast([128, 4, 64]),
                                   ALU.mult, [OB[q][1], cx.t("den")], [cx.t("osw")])

                        def swa_tr(j):
                            tr, trt = bk(7)
                            for q in range(4):
                                mm1(tr[:, q * 128:(q + 1) * 128], osw[:, 2 * q:2 * q + 2, :].rearrange("p a b -> p (a b)"), identb[:],
                                    [cx.t("osw"), cx.t("identb")], [trt], signal=(q == 3))

                        def swa_out(j):
                            tb = j // 4
                            tr, trt = bk(7)
                            cp(hmT[:, 2:6, 2 + j * 128:2 + (j + 1) * 128], tr[:].rearrange("p (q t) -> p q t", t=128), [trt],
                               [cx.t("hm", k, tb) for k in range(2, 6)], e="dve")

                        def recv_x1():
                            cx.dma("sp", xg[:], x1g[l].ap().rearrange("(r p) f -> p r f", p=128), reads=[cx.t("x1g", l)], writes=[cx.t("xg")])
                            xgt = cx.t("xg")
                            Sl = lambda r: xg[:, r, 0:256].rearrange("p (h c) -> p h c", c=64)
                            Dl = lambda r: xg[:, r, 256:260].unsqueeze(2).to_broadcast([128, 4, 64])
                            selt = cx.t("sel")
                            Sf = S_in[:].rearrange("p h c -> p (h c)")
                            Xf = X[:].rearrange("p h c -> p (h c)")
                            ts(Sf, xg[:, 0, 0:256], sel[:, 1:2], None, ALU.mult, None, [xgt, selt], [cx.t("S_in")])
                            tt(X[:], Sl(0), Dl(1), ALU.mult, [xgt], [cx.t("X")])
                            tt(X[:], X[:], Sl(1), ALU.add, [xgt, cx.t("X")], [cx.t("X")])
                            stt(Sf, Xf, sel[:, 2:3], Sf, ALU.mult, ALU.add, [cx.t("X"), cx.t("S_in"), selt], [cx.t("S_in")])
                            tt(X[:], X[:], Dl(2), ALU.mult, [xgt, cx.t("X")], [cx.t("X")])
                            tt(X[:], X[:], Sl(2), ALU.add, [xgt, cx.t("X")], [cx.t("X")])
                            stt(Sf, Xf, sel[:, 3:4], Sf, ALU.mult, ALU.add, [cx.t("X"), cx.t("S_in"), selt], [cx.t("S_in")])
                            hk, hkt = nexttmp()
                            ts(hk[:, 0:258], xg[:, 0, 260:518], sel[:, 4:5], None, ALU.mult, None, [xgt, selt], [hkt])
                            for r in range(1, 4):
                                stt(hk[:, 0:258], xg[:, r, 260:518], sel[:, 4 + r:5 + r], hk[:, 0:258], ALU.mult, ALU.add, [xgt, selt, hkt], [hkt])
                            cp(kTa[:, 0:128], hk[:, 0:128], [hkt], [cx.t("kTa_halo")])
                            cp(vaug[:, 0, :, :].rearrange("p a b -> p (a b)"), hk[:, 128:258], [hkt], [cx.t("vaug", 0)])

                        def qslot(j):
                            return 6 if j == 0 else j % 6

                        def swa_half(j):
                            if j == 3:
                                w_issue_upto(wstate["next"])
                            if j == 9:
                                w_issue_upto(wstate["next"] + 1)

                        swa_tiles = list(range(1, NT)) + [0]
                        pipeline([lambda j: swa_kp(j, qslot(j)), lambda j: swa_qk(j, qslot(j)), lambda j: swa_exp(j, qslot(j)), swa_pv, swa_norm,
                                  swa_tr, swa_out, swa_half], swa_tiles, loads=lambda j: load_q(j, qslot(j)), lead=4,
                                 hook=(len(swa_tiles) - 1, recv_x1))

                        cx.barrier()
                    a2 = [sb("a2_%d" % i, [128, 4, 128], BF, ph3) for i in range(6)]
                    sp2 = [sb("sp2_%d" % i, [128, 2, 4, 64], BF, ph3) for i in range(6)]
                    Sc = [sb("Sc%d" % i, [128, 2, 4, 64], BF, ph3) for i in range(2)]
                    o2 = [sb("o2_%d" % i, [128, 512], F32, ph3) for i in range(6)]
                    g2 = [sb("g2_%d" % i, [128, 512], BF, ph3) for i in range(4)]
                    ysq = [sb("ysq%d" % i, [128, 512], F32, ph3) for i in range(4)]
                    ssq = [sb("ssq%d" % i, [128, 8], F32, ph3) for i in range(4)]
                    mixt = [sb("mixt%d" % i, [128, 512], BF, ph3) for i in range(2)]

                    def load2(j):
                        s_ = j % 6
                        cx.dma("sp", a2[s_][:], a_s[:, :, j * 128:(j + 1) * 128].rearrange("h p t -> p h t"), reads=[cx.t("a_s", j // 4)],
                               writes=[cx.t("a2", s_)])
                        cx.dma("sp", sp2[s_][:].rearrange("p a h c -> p (a h c)"), sp_s[j, :, :], reads=[cx.t("sp_s", j)], writes=[cx.t("sp2", s_)])

                    def q0(j):
                        s = j % 2
                        tt(Sc[s][:], S_in[:].unsqueeze(1).to_broadcast([128, 2, 4, 64]),
                           etab[:, :, 2 * j:2 * j + 2].rearrange("p h n -> p n h").unsqueeze(3).to_broadcast([128, 2, 4, 64]), ALU.mult,
                           [cx.t("S_in"), cx.t("etab")], [cx.t("Sc", s)], e="pool")
                        cx.dma("sp", o2[j % 6][:], o_s[j, :, :], reads=[cx.t("o_s", j)], writes=[cx.t("o2", j % 6)])

                    def q1(j):
                        s = j % 2
                        s4 = j % 6
                        Cb = [bk(1 + 2 * s), bk(2 + 2 * s)]
                        for e2 in range(2):
                            er = slice(e2 * 64, (e2 + 1) * 64)
                            for ch in range(2):
                                cc = slice(ch * 64, (ch + 1) * 64)
                                for hp in range(4):
                                    oo = Cb[e2][0][ch * 64:(ch + 1) * 64, hp * 64:(hp + 1) * 64]
                                    mm1(oo, a2[s4][er, hp, cc], sp2[s4][er, ch, hp, :], [cx.t("a2", s4), cx.t("sp2", s4)], [Cb[e2][1]],
                                        start=True, stop=False, signal=False)
                                    mm1(oo, a2[s4][er, hp, cc], Sc[s][er, ch, hp, :], [cx.t("a2", s4), cx.t("Sc", s)], [Cb[e2][1]],
                                        start=False, stop=True, signal=(ch == 1 and hp == 3))

                    def q2(j):
                        s = j % 2
                        s6 = j % 6
                        Cb = [bk(1 + 2 * s), bk(2 + 2 * s)]
                        ov = o2[s6][:].rearrange("p (h e c) -> p h e c", e=2, c=64)
                        for e2 in range(2):
                            tt(ov[:, :, e2, :], ov[:, :, e2, :], Cb[e2][0][:, 0:256].rearrange("p (h c) -> p h c", c=64), ALU.add,
                               [cx.t("o2", s6), Cb[e2][1]], [cx.t("o2", s6)])

                    def q3(j):
                        act(ysq[j % 4][:], o2[j % 6][:], AF.Square, [cx.t("o2", j % 6)], [cx.t("ysq", j % 4)])
                        cx.dma("sp", g2[j % 4][:], g_s[j, :, :], reads=[cx.t("g_s", j)], writes=[cx.t("g2", j % 4)])

                    def q4(j):
                        y4 = j % 4
                        cx.op("dve", lambda g: g.reduce_sum(out=ssq[y4][:], in_=ysq[y4][:].rearrange("p (h c) -> p h c", c=64), axis=AX.X),
                              reads=[cx.t("ysq", y4)], writes=[cx.t("ssq", y4)])

                    def q5(j):
                        y4 = j % 4
                        act(ssq[y4][:], ssq[y4][:], AF.Ln, [cx.t("ssq", y4), cx.t("epsc")], [cx.t("ssq", y4)], scale=1.0 / 64.0, bias=epsc[:, 0:1])
                        act(ssq[y4][:], ssq[y4][:], AF.Exp, [cx.t("ssq", y4)], [cx.t("ssq", y4)], scale=-0.5)

                    def q6(j):
                        y4 = j % 4
                        s6 = j % 6
                        tt(ysq[y4][:].rearrange("p (h c) -> p h c", c=64), o2[s6][:].rearrange("p (h c) -> p h c", c=64),
                           ssq[y4][:].unsqueeze(2).to_broadcast([128, 8, 64]), ALU.mult, [cx.t("o2", s6), cx.t("ssq", y4)], [cx.t("ysq", y4)])

                    def q7(j):
                        s = j % 2
                        tt(mixt[s][:], ysq[j % 4][:], g2[j % 4][:], ALU.mult, [cx.t("ysq", j % 4), cx.t("g2", j % 4)], [cx.t("mixt", s)], e="pool")

                    def q8(j):
                        s = j % 2
                        tr, trt = bk(5 + s)
                        for q in range(4):
                            mm1(tr[:, q * 128:(q + 1) * 128], mixt[s][:, q * 128:(q + 1) * 128], identb[:], [cx.t("mixt", s), cx.t("identb")],
                                [trt], signal=(q == 3))

                    def q9(j):
                        s = j % 2
                        tb = j // 4
                        tr, trt = bk(5 + s)
                        trv = tr[:].rearrange("p (q t) -> p q t", t=128)
                        cp(hmT[:, 0:2, 2 + j * 128:2 + (j + 1) * 128], trv[:, 0:2, :], [trt], [cx.t("hm", 0, tb), cx.t("hm", 1, tb)], e="act")
                        cp(hmT[:, 6:8, 2 + j * 128:2 + (j + 1) * 128], trv[:, 2:4, :], [trt], [cx.t("hm", 6, tb), cx.t("hm", 7, tb)], e="act")

                    pipeline([q0, q1, q2, q3, q4, q5, q6, q7, q8, q9], list(range(NT)), loads=load2, lead=4)
                    cx.barrier()
                cx.barrier()
            wo = [w_next(8, 512, 2), w_next(8, 512, 1)]
            for tb in (3, 0, 1, 2):
                for m in range(8):
                    w, wt = wo[m // 4]
                    mm_ = m % 4
                    b_, bt_ = bk(m % 4)
                    mm_group(b_[:], [(w[:, k, mm_ * 128:(mm_ + 1) * 128], hmT[:, k, 2 + tb * 512:2 + (tb + 1) * 512]) for k in range(8)],
                             hm_trks(tb) + [wt], bt_)
                    tt(rT[:, m, tb * 512:(tb + 1) * 512], rT[:, m, tb * 512:(tb + 1) * 512], b_[:], ALU.add, [cx.t("rT", m, tb), bt_],
                       [cx.t("rT", m, tb)])
                if tb == 3:
                    act(sq2[:], rT[:, :, T - 2:T], AF.Square, rT_trks(3), [cx.t("sq2")])
                    b_, bt_ = bk(4)
                    for k in range(8):
                        mm1(b_[:, 0:2], onesb[:], sq2[:, k, :], [cx.t("sq2"), cx.t("onesb")], [bt_] if (k == 0 or k == 7) else (),
                            start=(k == 0), stop=(k == 7), signal=(k == 7))
                    act(rs2[:], b_[:, 0:2], AF.Ln, [bt_, cx.t("epsc")], [cx.t("rs2")], scale=1.0 / D, bias=epsc[:, 0:1])
                    act(rs2[:], rs2[:], AF.Exp, [cx.t("rs2")], [cx.t("rs2")], scale=-0.5)
                    x2v = xs2[:].rearrange("p (k c) -> p k c", c=2)
                    tt(x2v, rT[:, :, T - 2:T], gains[:, G_FFN:G_FFN + 8].unsqueeze(2).to_broadcast([128, 8, 2]), ALU.mult,
                       rT_trks(3) + [cx.t("gains")], [cx.t("xs2")])
                    tt(x2v, x2v, rs2[:].unsqueeze(1).to_broadcast([128, 8, 2]), ALU.mult, [cx.t("xs2"), cx.t("rs2")], [cx.t("xs2")])
                    cx.dma("sp", x2s[l][:, :], xs2[:], reads=[cx.t("xs2")], writes=[cx.t("x2s", l)])
                    cx.collective(x2s[l].ap().opt(), x2g[l].ap().opt(), GROUPS, reads=[cx.t("x2s", l)], writes=[cx.t("x2g", l)])
            if dump(l, 1):
                stopped = True
                break

            norm_to_hm(G_FFN)
            with ExitStack() as ph:
                xg2 = sb("xg2", [128, 4, 16], F32, ph)
                hh = sb("hh", [128, 16], F32, ph)
                cx.dma("sp", xg2[:], x2g[l].ap().rearrange("(r p) f -> p r f", p=128), reads=[cx.t("x2g", l)], writes=[cx.t("xg2")])
                ts(hh[:], xg2[:, 0, :], sel[:, 4:5], None, ALU.mult, None, [cx.t("xg2"), cx.t("sel")], [cx.t("hh")])
                for r in range(1, 4):
                    stt(hh[:], xg2[:, r, :], sel[:, 4 + r:5 + r], hh[:], ALU.mult, ALU.add, [cx.t("xg2"), cx.t("sel"), cx.t("hh")], [cx.t("hh")])
                cp(hmT[:, :, 0:2], hh[:].rearrange("p (k c) -> p k c", c=2), [cx.t("hh")], [cx.t("hm_halo")])

                actT = sb("actT", [128, 11, T], BF, ph)
                accs = [sb("accs%d" % i, [128, 512], F32, ph) for i in range(3)]
                tails = [sb("tail%d" % i, [128, 2], F32, ph) for i in range(3)]
                a1 = [sb("a1_%d" % i, [128, 512], BF, ph) for i in range(2)]
                cidx = 0
                for hf in range(2):
                    c0 = hf * 11
                    for cpi in range(6):
                        ncc = 2 if cpi < 5 else 1
                        w, wt = w_next(8, ncc * 256)
                        for ci in range(ncc):
                            c = c0 + 2 * cpi + ci
                            cl = c - c0
                            cw = lambda jj: convw[:, (l * NCH + c) * 3 + jj:(l * NCH + c) * 3 + jj + 1]
                            wg_ = lambda k: w[:, k, ci * 256:ci * 256 + 128]
                            wu_ = lambda k: w[:, k, ci * 256 + 128:ci * 256 + 256]
                            hb, hbt = bk(4)
                            mm_group(hb[:, 0:2], [(wg_(k), hmT[:, k, 0:2]) for k in range(8)], [cx.t("hm_halo"), wt], hbt)
                            ti = cidx % 3
                            cidx += 1
                            cp(tails[ti][:], hb[:, 0:2], [hbt], [cx.t("tail", ti)], e="act")
                            for tb in range(NB):
                                gb, gbt = bk(tb % 2)
                                mm_group(gb[:], [(wg_(k), hmT[:, k, 2 + tb * 512:2 + (tb + 1) * 512]) for k in range(8)], hm_trks(tb) + [wt], gbt)
                                ub, ubt = bk(2 + tb % 2)
                                mm_group(ub[:], [(wu_(k), hmT[:, k, 2 + tb * 512:2 + (tb + 1) * 512]) for k in range(8)], hm_trks(tb) + [wt], ubt)
                                ai = cidx % 3
                                acc, acct = accs[ai], cx.t("accs", ai)
                                act(acc[:], gb[:], AF.Identity, [gbt, cx.t("convw"), cx.t("convb")], [acct], scale=cw(2),
                                    bias=convb[:, l * NCH + c:l * NCH + c + 1])
                                stt(acc[:, 1:512], gb[:, 0:511], cw(1), acc[:, 1:512], ALU.mult, ALU.add, [gbt, acct, cx.t("convw")], [acct])
                                stt(acc[:, 2:512], gb[:, 0:510], cw(0), acc[:, 2:512], ALU.mult, ALU.add, [gbt, acct, cx.t("convw")], [acct])
                                tlt = cx.t("tail", ti)
                                stt(acc[:, 0:1], tails[ti][:, 1:2], cw(1), acc[:, 0:1], ALU.mult, ALU.add, [tlt, acct, cx.t("convw")], [acct])
                                stt(acc[:, 0:2], tails[ti][:, 0:2], cw(0), acc[:, 0:2], ALU.mult, ALU.add, [tlt, acct, cx.t("convw")], [acct])
                                if tb < NB - 1:
                                    ti = cidx % 3
                                    cidx += 1
                                    cp(tails[ti][:], gb[:, 510:512], [gbt], [cx.t("tail", ti)], e="act")
                                else:
                                    cidx += 1
                                sa = tb % 2
                                act(a1[sa][:], acc[:], AF.Gelu_apprx_tanh, [acct], [cx.t("a1", sa)])
                                tt(actT[:, cl, tb * 512:(tb + 1) * 512], a1[sa][:], ub[:], ALU.mult, [cx.t("a1", sa), ubt], [cx.t("actT", cl, tb)])
                    for mq in range(4):
                        w, wt = w_next(11, 256)
                        for mm_ in range(2):
                            m = mq * 2 + mm_
                            for tb in range(NB):
                                b_, bt_ = bk(6 + tb % 2)
                                mm_group(b_[:], [(w[:, cl, mm_ * 128:(mm_ + 1) * 128], actT[:, cl, tb * 512:(tb + 1) * 512]) for cl in range(11)],
                                         [cx.t("actT", cl, tb) for cl in range(11)] + [wt], bt_)
                                tt(rT[:, m, tb * 512:(tb + 1) * 512], rT[:, m, tb * 512:(tb + 1) * 512], b_[:], ALU.add, [cx.t("rT", m, tb), bt_],
                                   [cx.t("rT", m, tb)])
                cx.barrier()
            if dump(l, 2):
                stopped = True
                break

            with ExitStack() as ph:
                pT = sb("pT", [128, 2, T], BF, ph)
                pb = [sb("pb%d" % i, [128, 256], F32, ph) for i in range(2)]
                sg = [sb("sg%d" % i, [128, 512], F32, ph) for i in range(2)]
                for tb in range(NB):
                    pbk = [bk(0), bk(1)]
                    for jj in range(4):
                        j = tb * 4 + jj
                        s = j % 2
                        cx.dma("sp", pb[s][:], p_d[l, j * 128:(j + 1) * 128, :], writes=[cx.t("pb", s)])
                        for k2 in range(2):
                            mm1(pbk[k2][0][:, jj * 128:(jj + 1) * 128], pb[s][:, k2 * 128:(k2 + 1) * 128], identf, [cx.t("pb", s), cx.t("cst")],
                                [pbk[k2][1]], signal=True)
                    for k2 in range(2):
                        cp(pT[:, k2, tb * 512:(tb + 1) * 512], pbk[k2][0][:], [pbk[k2][1]], [cx.t("pT", k2, tb)], e=("act" if k2 else "dve"))
                norm_to_hm(G_PLE)
                wq = [w_next(8, 512, 2), w_next(8, 512, 1)]
                wpp, wppt = w_next(2, 1024, 0)
                for u in range(2):
                    w, wt = wq[u]
                    for mm_ in range(4):
                        m = u * 4 + mm_
                        for tb in range(NB):
                            s = tb % 2
                            gb, gbt = bk(2 + s)
                            mm_group(gb[:], [(w[:, k, mm_ * 128:(mm_ + 1) * 128], hmT[:, k, 2 + tb * 512:2 + (tb + 1) * 512]) for k in range(8)],
                                     hm_trks(tb) + [wt], gbt)
                            ppb, ppbt = bk(4 + s)
                            mm_group(ppb[:], [(wpp[:, k2, m * 128:(m + 1) * 128], pT[:, k2, tb * 512:(tb + 1) * 512]) for k2 in range(2)],
                                     [cx.t("pT", 0, tb), cx.t("pT", 1, tb), wppt], ppbt)
                            act(sg[s][:], gb[:], AF.Sigmoid, [gbt], [cx.t("sg", s)])
                            tt(sg[s][:], sg[s][:], ppb[:], ALU.mult, [cx.t("sg", s), ppbt], [cx.t("sg", s)])
                            tt(rT[:, m, tb * 512:(tb + 1) * 512], rT[:, m, tb * 512:(tb + 1) * 512], sg[s][:], ALU.add,
                               [cx.t("rT", m, tb), cx.t("sg", s)], [cx.t("rT", m, tb)])
                cx.barrier()
            if dump(l, 3):
                stopped = True
                break

        if not stopped:
            with ExitStack() as ph:
                yT = sb("yT", [128, 8, 512], F32, ph)
                ot = [sb("ot%d" % i, [128, D], F32, ph) for i in range(2)]
                GF = DEPTH * 3 * 8
                for tb in range(NB):
                    rs, rst = rstd_block2(tb, 0)
                    for k in range(8):
                        stt(yT[:, k, :], rT[:, k, tb * 512:(tb + 1) * 512], gains[:, GF + k:GF + k + 1], rs[:], ALU.mult, ALU.mult,
                            [cx.t("rT", k, tb), rst, cx.t("gains")], [cx.t("yT", k)])
                    for jj in range(4):
                        j = tb * 4 + jj
                        s = j % 2
                        for half in range(2):
                            b_, bt_ = bk(1 + half + 2 * (j % 2))
                            for q in range(4):
                                m = half * 4 + q
                                mm1(b_[:, q * 128:(q + 1) * 128], yT[:, m, jj * 128:(jj + 1) * 128], identf, [cx.t("yT", m), cx.t("cst")], [bt_],
                                    signal=(q == 3))
                            cp(ot[s][:, half * 512:(half + 1) * 512], b_[:], [bt_], [cx.t("ot", s)], e=("act" if half else "dve"))
                        cx.dma("sp", out_d[j * 128:(j + 1) * 128, :], ot[s][:], reads=[cx.t("ot", s)], writes=[cx.t("out")])
        cx.barrier(final=True)
        for i in range(cx.NDMA):
            n = "d%d" % i
            if cx.cnt[n] > 0:
                cx._wait("sp", [(n, cx.cnt[n])])
        if cx.cnt["cc"] > 0:
            cx._wait("sp", [("cc", cx.cnt["cc"])])
    return nc


def _swap_idx(base, ncols):
    j = np.arange(ncols)
    return base + (j // 64) * 64 + ((j % 64) + 32) % 64


def _in_cols():
    aq, af, ai, ag, bq, bk_, bv, cq, ck, cv, cg = 0, 256, 512, 768, 1024, 1536, 1664, 1792, 2048, 2304, 2560
    r = lambda b, n: b + np.arange(n)
    cols = []
    cols += [r(aq, 128), r(af, 128), r(aq + 128, 128), r(af + 128, 128)]
    for hp in range(2):
        cols += [r(cq + hp * 128, 128), r(ck + hp * 128, 128)]
    for c in range(4):
        cols += [np.concatenate([r(bq + c * 64, 64), r(bq + (4 + c) * 64, 64)])]
    cols += [r(bk_, 128)]
    cols += [r(ai, 256), r(cv, 256)]
    cols += [r(ag, 256), r(cg, 256)]
    cols += [r(bv, 128)]
    idx = np.concatenate(cols)
    assert idx.shape[0] == NIN
    return idx


def _consts():
    c = np.zeros((128, NCONST), np.float32)
    p = np.arange(128)
    c[:, C_ID:C_ID + 128] = np.eye(128, dtype=np.float32)
    t = np.arange(64)
    c[:, C_CAUS:C_CAUS + 64] = ((p[:, None] % 64) <= t[None, :]).astype(np.float32)
    q = np.arange(128)
    mb = np.zeros((128, 2, 128), np.float32)
    mb[:, 0, :] = np.where(p[:, None] > q[None, :], 0.0, NEG)
    mb[:, 1, :] = np.where(p[:, None] <= q[None, :], 0.0, NEG)
    c[:, C_MB:C_MB + 256] = mb.reshape(128, 256)
    seg = np.ones(512, np.float32)
    seg[::64] = 0.0
    c[:, C_SEG:C_SEG + 512] = seg[None, :]
    inv = 1.0 / (10000.0 ** (np.arange(0, 64, 2, dtype=np.float32) / np.float32(64)))
    inv = inv.astype(np.float32)
    c[:, C_INV] = inv[(p % 64) % 32]
    c[:, C_SIGN] = np.where((p % 64) < 32, -1.0, 1.0)
    tau = np.arange(64, dtype=np.float64)
    for hp in range(2):
        h = 2 * hp + p // 64
        lg = np.log(1.0 - 2.0 ** (-5.0 - h.astype(np.float64)))
        c[:, C_EAC + hp * 64:C_EAC + (hp + 1) * 64] = np.exp((tau[None, :] - 31.0) * lg[:, None])
        c[:, C_EBC + hp * 64:C_EBC + (hp + 1) * 64] = np.exp((31.0 - tau[None, :]) * lg[:, None]) * 0.125
        c[:, C_LNC + hp] = 32.0 * lg
        c[:, C_EC + hp] = np.exp(32.0 * lg)
    sig = (p // 64) * 64 + ((p % 64) + 32) % 64
    pm = np.zeros((128, 128), np.float32)
    pm[sig, p] = 1.0
    c[:, C_PM:C_PM + 128] = pm
    return c


def _fm(w):
    K, N = w.shape
    return np.ascontiguousarray(w.reshape(K // 128, 128, N).transpose(1, 0, 2))


def _prep_inputs(x, p, positions, attn_norm, w_in, hgrn_lb, hgrn_gnorm, attn_sinks, w_out, ffn_norm, w_gate, w_up, conv_w,
                 conv_b, w_down, ple_norm, w_ple_gate, w_ple_proj, final_norm):
    f32 = lambda a: np.ascontiguousarray(np.asarray(a, dtype=np.float32))
    x, p = f32(x), f32(p)
    positions = np.asarray(positions).astype(np.int32)
    idx = _in_cols()
    w_in_r = np.stack([_fm(f32(w_in[l])[:, idx]) for l in range(DEPTH)])
    w_out_r = np.stack([_fm(f32(w_out[l])) for l in range(DEPTH)])
    wgu = []
    for l in range(DEPTH):
        g = _fm(f32(w_gate[l])).reshape(128, 8, NCH, 1, 128)
        u = _fm(f32(w_up[l])).reshape(128, 8, NCH, 1, 128)
        wgu.append(np.concatenate([g, u], axis=3).reshape(128, 8, 2 * DFF))
    w_gu_r = np.stack(wgu)
    w_dn_r = np.stack([_fm(f32(w_down[l])) for l in range(DEPTH)])
    w_pg_r = np.stack([_fm(f32(w_ple_gate[l])) for l in range(DEPTH)])
    w_pp_r = np.stack([_fm(f32(w_ple_proj[l])) for l in range(DEPTH)])
    gl = []
    for l in range(DEPTH):
        for a in (attn_norm, ffn_norm, ple_norm):
            gl.append(f32(a[l]).reshape(8, 128).T)
    gl.append(f32(final_norm).reshape(8, 128).T)
    gains = np.ascontiguousarray(np.concatenate(gl, axis=1))
    lb = np.ascontiguousarray(f32(hgrn_lb).reshape(DEPTH, 2, 128).transpose(2, 1, 0).reshape(128, 2 * DEPTH))
    gn = f32(hgrn_gnorm).reshape(1, DEPTH * 256)
    sinks = f32(attn_sinks).reshape(1, DEPTH * 8)
    cw = np.ascontiguousarray(f32(conv_w).reshape(DEPTH, 3, NCH, 128).transpose(3, 0, 2, 1).reshape(128, DEPTH * NCH * 3))
    cb = np.ascontiguousarray(f32(conv_b).reshape(DEPTH, NCH, 128).transpose(2, 0, 1).reshape(128, DEPTH * NCH))
    consts = _consts()
    shared = dict(consts=consts, w_in=w_in_r, w_out=w_out_r, w_gu=w_gu_r, w_dn=w_dn_r, w_pg=w_pg_r, w_pp=w_pp_r, gains=gains, lb=lb,
                  gnorm=gn, sinks=sinks, convw=cw, convb=cb)
    in_maps = []
    for c in range(NCORES):
        b, rho = c // 4, c % 4
        sl = slice(rho * T, (rho + 1) * T)
        sel = np.zeros((128, 8), np.float32)
        sel[:, rho] = 1.0
        if rho > 0:
            sel[:, 4 + rho - 1] = 1.0
        m = dict(shared)
        m["x"] = np.ascontiguousarray(x[b, sl, :])
        m["p"] = np.ascontiguousarray(p[:, b, sl, :])
        m["pos"] = np.ascontiguousarray(positions[b, sl].reshape(1, T))
        m["sel"] = sel
        in_maps.append(m)
    return in_maps


_NC_CACHE = {}


def kernel(x, p, positions, attn_norm, w_in, hgrn_lb, hgrn_gnorm, attn_sinks, w_out, ffn_norm, w_gate, w_up, conv_w, conv_b,
           w_down, ple_norm, w_ple_gate, w_ple_proj, final_norm):
    in_maps = _prep_inputs(x, p, positions, attn_norm, w_in, hgrn_lb, hgrn_gnorm, attn_sinks, w_out, ffn_norm, w_gate, w_up,
                           conv_w, conv_b, w_down, ple_norm, w_ple_gate, w_ple_proj, final_norm)
    nc = build_program()
    res = run_bass_kernel_spmd(nc, in_maps, core_ids=list(range(NCORES)))
    out = np.zeros((2, 4 * T, D), np.float32)
    for c in range(NCORES):
        b, rho = c // 4, c % 4
        out[b, rho * T:(rho + 1) * T, :] = res.results[c]["out"]
    return out
```

```python
from contextlib import ExitStack
import math
import numpy as np
import concourse.bass as bass
import concourse.mybir as mybir
from concourse.bass_utils import run_bass_kernel_spmd

F32 = mybir.dt.float32
BF = mybir.dt.bfloat16
I32 = mybir.dt.int32
AF = mybir.ActivationFunctionType
ALU = mybir.AluOpType
AX = mybir.AxisListType

NCORES = 8
T = 2048
NT = 16
NB = 4
D = 1024
DFF = 2816
NCH = 22
DEPTH = 4
NIN = 2816
EPS = 1e-6
NEG = -30000.0
XW = 520

C_ID = 0
C_CAUS = 128
C_MB = 192
C_SEG = 448
C_INV = 960
C_SIGN = 961
C_EAC = 962
C_EBC = 1090
C_LNC = 1218
C_EC = 1220
C_PM = 1224
NCONST = 1352


class Trk:
    __slots__ = ("w", "r")

    def __init__(self):
        self.w = None
        self.r = {}


class Ctx:
    NDMA = 24

    def __init__(self, nc, es):
        self.nc = nc
        self.eng = {"pe": nc.tensor, "dve": nc.vector, "act": nc.scalar, "pool": nc.gpsimd, "sp": nc.sync}
        self.sem = {}
        self.mult = {}
        for e in self.eng:
            self.sem[e] = es.enter_context(nc.semaphore("s_" + e))
            self.mult[e] = 1
        for i in range(self.NDMA):
            n = "d%d" % i
            self.sem[n] = es.enter_context(nc.semaphore("s_" + n))
            self.mult[n] = 16
        self.sem["cc"] = es.enter_context(nc.semaphore("s_cc"))
        self.mult["cc"] = 1
        self.cnt = {k: 0 for k in self.sem}
        self.waited = {e: {} for e in self.eng}
        self.dma_i = 0
        self.dma_pi = 0
        self.trk = {}

    def t(self, *key):
        k = tuple(key)
        v = self.trk.get(k)
        if v is None:
            v = Trk()
            self.trk[k] = v
        return v

    def _wait(self, e, toks):
        need = {}
        for (src, c) in toks:
            if need.get(src, 0) < c:
                need[src] = c
        w = self.waited[e]
        for src, c in need.items():
            if w.get(src, 0) < c:
                self.eng[e].wait_ge(self.sem[src], c * self.mult[src])
                w[src] = c

    def _deps(self, e, reads, writes):
        deps = []
        for t in reads:
            if t.w is not None and not (e == "pe" and t.w[0] == "pe"):
                deps.append(t.w)
        for t in writes:
            if t.w is not None and not (e == "pe" and t.w[0] == "pe"):
                deps.append(t.w)
            for src, c in t.r.items():
                if not (e == "pe" and src == "pe"):
                    deps.append((src, c))
        return deps

    def _record(self, tok, reads, writes):
        for t in reads:
            if t.r.get(tok[0], 0) < tok[1]:
                t.r[tok[0]] = tok[1]
        for t in writes:
            t.w = tok
            t.r = {}

    def op(self, e, fn, reads=(), writes=(), signal=True):
        self._wait(e, self._deps(e, reads, writes))
        ins = fn(self.eng[e])
        if signal:
            self.cnt[e] += 1
            ins.then_inc(self.sem[e], 1)
            tok = (e, self.cnt[e])
        else:
            tok = (e, self.cnt[e] + 1)
        self._record(tok, reads, writes)
        return tok

    def dma(self, e, out, in_, reads=(), writes=()):
        if e == "pool":
            i = self.dma_pi % 4
            self.dma_pi += 1
        else:
            i = 4 + self.dma_i % (self.NDMA - 4)
            self.dma_i += 1
        n = "d%d" % i
        deps = self._deps(e, reads, writes)
        if self.cnt[n] > 0:
            deps.append((n, self.cnt[n]))
        self._wait(e, deps)
        self.cnt[n] += 1
        self.eng[e].dma_start(out=out, in_=in_).then_inc(self.sem[n], 16)
        tok = (n, self.cnt[n])
        self._record(tok, reads, writes)
        return tok

    def collective(self, src_ap, dst_ap, groups, reads=(), writes=()):
        e = "pool"
        self._wait(e, self._deps(e, reads, writes))
        self.cnt["cc"] += 1
        self.nc.gpsimd.collective_compute("AllGather", ALU.bypass, replica_groups=groups,
                                          ins=[src_ap], outs=[dst_ap]).then_inc(self.sem["cc"])
        tok = ("cc", self.cnt["cc"])
        self._record(tok, reads, writes)
        return tok

    def barrier(self, final=False):
        waiters = ("dve", "act", "sp", "pool") + (("pe",) if final else ())
        toks_eng = [(o, self.cnt[o]) for o in ("pe", "dve", "act", "pool") if self.cnt[o] > 0]
        toks_dma = [(n, c) for n, c in self.cnt.items() if n.startswith("d") and n != "dve" and c > 0]
        for e in waiters:
            self._wait(e, [t for t in toks_eng if t[0] != e] + toks_dma)


def build_program(n_layers=DEPTH, stop=None):
    nc = bass.Bass("TRN2", target_bir_lowering=False)
    dt = nc.dram_tensor
    x_d = dt("x", [T, D], F32, kind="ExternalInput")
    p_d = dt("p", [DEPTH, T, 256], F32, kind="ExternalInput")
    pos_d = dt("pos", [1, T], I32, kind="ExternalInput")
    sel_d = dt("sel", [128, 8], F32, kind="ExternalInput")
    consts_d = dt("consts", [128, NCONST], F32, kind="ExternalInput")
    win_d = dt("w_in", [DEPTH, 128, 8, NIN], F32, kind="ExternalInput")
    wout_d = dt("w_out", [DEPTH, 128, 8, D], F32, kind="ExternalInput")
    wgu_d = dt("w_gu", [DEPTH, 128, 8, 2 * DFF], F32, kind="ExternalInput")
    wdn_d = dt("w_dn", [DEPTH, 128, NCH, D], F32, kind="ExternalInput")
    wpg_d = dt("w_pg", [DEPTH, 128, 8, D], F32, kind="ExternalInput")
    wpp_d = dt("w_pp", [DEPTH, 128, 2, D], F32, kind="ExternalInput")
    gains_d = dt("gains", [128, DEPTH * 3 * 8 + 8], F32, kind="ExternalInput")
    lb_d = dt("lb", [128, 2 * DEPTH], F32, kind="ExternalInput")
    gn_d = dt("gnorm", [1, DEPTH * 256], F32, kind="ExternalInput")
    sink_d = dt("sinks", [1, DEPTH * 8], F32, kind="ExternalInput")
    cw_d = dt("convw", [128, DEPTH * NCH * 3], F32, kind="ExternalInput")
    cb_d = dt("convb", [128, DEPTH * NCH], F32, kind="ExternalInput")
    out_d = dt("out", [T, D], F32, kind="ExternalOutput")
    dbg_d = dt("dbg", [128, 8, T], F32, kind="ExternalOutput") if stop is not None else None
    g_s = dt("g_s", [NT, 128, 512], BF)
    a_s = dt("a_s", [4, 128, T], BF)
    b_s = dt("b_s", [4, 128, T], BF)
    q_s = dt("q_s", [4, 128, T], BF)
    sp_s = dt("sp_s", [NT, 128, 512], BF)
    o_s = dt("o_s", [NT, 128, 512], F32)
    tab_s = dt("tab_s", [2, 128, T], F32)
    x1s = [dt("x1s%d" % l, [128, XW], F32) for l in range(DEPTH)]
    x1g = [dt("x1g%d" % l, [4 * 128, XW], F32) for l in range(DEPTH)]
    x2s = [dt("x2s%d" % l, [128, 16], F32) for l in range(DEPTH)]
    x2g = [dt("x2g%d" % l, [4 * 128, 16], F32) for l in range(DEPTH)]
    GROUPS = [[0, 1, 2, 3], [4, 5, 6, 7]]

    with ExitStack() as es:
        cx = Ctx(nc, es)
        uid = [0]

        def sb(name, shape, dtype, st=es):
            uid[0] += 1
            return st.enter_context(nc.sbuf_tensor("sb%d_%s" % (uid[0], name), shape, dtype))
        rT = sb("rT", [128, 8, T], F32)
        hmT = sb("hmT", [128, 8, T + 2], BF)
        ring = sb("ring", [128, 3, 4096], BF)
        cst = sb("cst", [128, NCONST], F32)
        identb = sb("identb", [128, 128], BF)
        permb = sb("permb", [128, 128], BF)
        maskb = sb("maskb", [128, 2, 128], BF)
        onesf = sb("onesf", [128, 128], F32)
        onesb = sb("onesb", [128, 128], BF)
        sqb = [sb("sqb%d" % i, [128, 2, 512], BF) for i in range(2)]
        gains = sb("gains", [128, DEPTH * 3 * 8 + 8], F32)
        lbraw = sb("lbraw", [128, 2, DEPTH], F32)
        lbv = sb("lbv", [128, 2, DEPTH], F32)
        oml = sb("oml", [128, 2, DEPTH], F32)
        esink = sb("esink", [128, DEPTH * 8], F32)
        convw = sb("convw", [128, DEPTH * NCH * 3], F32)
        convb = sb("convb", [128, DEPTH * NCH], F32)
        sel = sb("sel", [128, 8], F32)
        lnm = sb("lnm", [128, 4, 32], F32)
        lnd = sb("lnd", [128, 4, 32], F32)
        mtab = sb("mtab", [128, 4, 32], F32)
        dtab = sb("dtab", [128, 4, 32], F32)
        etab = sb("etab", [128, 4, 32], F32)
        cstab = sb("cstab", [128, 4, 32], F32)
        ctab = sb("ctab", [128, 4, 32], F32)
        dtot = sb("dtot", [128, 4], F32)
        tmp = [sb("tmp%d" % i, [128, 512], F32) for i in range(6)]
        banks = [es.enter_context(nc.psum_tensor("bank%d" % i, [128, 512], F32)) for i in range(8)]

        identf = cst[:, C_ID:C_ID + 128]
        caus = cst[:, C_CAUS:C_CAUS + 64]

        tmp_i = [0]

        def nexttmp():
            i = tmp_i[0] % len(tmp)
            tmp_i[0] += 1
            return tmp[i], cx.t("tmp", i)

        def bk(i):
            return banks[i], cx.t("bank", i)

        def mm_group(out_ap, pairs, reads, wtrk, extra_reads_last=()):
            n = len(pairs)
            for i, (l, r) in enumerate(pairs):
                last = i == n - 1
                cx.op("pe", lambda e, l=l, r=r, i=i, last=last: e.matmul(out_ap, lhsT=l, rhs=r, start=(i == 0), stop=last),
                      reads=reads if i == 0 else (), writes=[wtrk] if (i == 0 or last) else (), signal=last)
            tok = ("pe", cx.cnt["pe"])
            for t in reads:
                if t.r.get("pe", 0) < tok[1]:
                    t.r["pe"] = tok[1]

        def mm1(out_ap, l, r, reads, writes, start=True, stop=True, signal=True, tr=False):
            if tr:
                cx.op("pe", lambda e: e.matmul(out_ap, lhsT=l, rhs=r, start=start, stop=stop, is_transpose=True), reads=reads,
                      writes=writes, signal=signal)
            else:
                cx.op("pe", lambda e: e.matmul(out_ap, lhsT=l, rhs=r, start=start, stop=stop), reads=reads, writes=writes,
                      signal=signal)

        def act(out, in_, func, reads, writes, scale=1.0, bias=0.0):
            cx.op("act", lambda e: e.activation(out=out, in_=in_, func=func, bias=bias, scale=scale), reads=reads, writes=writes)

        def tt(out, in0, in1, op, reads, writes, e="dve"):
            cx.op(e, lambda g: g.tensor_tensor(out=out, in0=in0, in1=in1, op=op), reads=reads, writes=writes)

        def ts(out, in0, s1, s2, op0, op1, reads, writes, e="dve"):
            if s2 is None:
                cx.op(e, lambda g: g.tensor_scalar(out=out, in0=in0, scalar1=s1, scalar2=None, op0=op0), reads=reads, writes=writes)
            else:
                cx.op(e, lambda g: g.tensor_scalar(out=out, in0=in0, scalar1=s1, scalar2=s2, op0=op0, op1=op1), reads=reads,
                      writes=writes)

        def stt(out, in0, scalar, in1, op0, op1, reads, writes):
            cx.op("dve", lambda g: g.scalar_tensor_tensor(out=out, in0=in0, scalar=scalar, in1=in1, op0=op0, op1=op1),
                  reads=reads, writes=writes)

        def cp(out, in_, reads, writes, e="dve"):
            if e == "act":
                cx.op("act", lambda g: g.copy(out=out, in_=in_), reads=reads, writes=writes)
            else:
                cx.op(e, lambda g: g.tensor_copy(out=out, in_=in_), reads=reads, writes=writes)

        def memset(ap, val, writes, e="dve"):
            cx.op(e, lambda g: g.memset(ap, val), writes=writes)

        def pipeline(stages, tiles, loads=None, lead=2, hook=None):
            n = len(stages)
            if loads is not None:
                for t_ in tiles[:lead]:
                    loads(t_)
            for i in range(len(tiles) + n - 1):
                if hook is not None and i == hook[0]:
                    hook[1]()
                for k in reversed(range(n)):
                    idx = i - k
                    if 0 <= idx < len(tiles):
                        stages[k](tiles[idx])
                if loads is not None and i + lead < len(tiles):
                    loads(tiles[i + lead])

        wunits = []

        def add_unit(ap3, nk, ncol):
            wunits.append((ap3, nk, ncol))

        for l in range(n_layers):
            for u in range(3):
                add_unit(win_d[l, :, :, u * 512:(u + 1) * 512], 8, 512)
            add_unit(win_d[l, :, :, 1536:1664], 8, 128)
            add_unit(win_d[l, :, :, 1664:2176], 8, 512)
            add_unit(win_d[l, :, :, 2176:2688], 8, 512)
            add_unit(win_d[l, :, :, 2688:2816], 8, 128)
            for u in range(2):
                add_unit(wout_d[l, :, :, u * 512:(u + 1) * 512], 8, 512)
            for hf in range(2):
                c0 = hf * 11
                for cpi in range(6):
                    c = c0 + 2 * cpi
                    ncc = 2 if cpi < 5 else 1
                    add_unit(wgu_d[l, :, :, c * 256:(c + ncc) * 256], 8, ncc * 256)
                for mq in range(4):
                    add_unit(wdn_d[l, :, c0:c0 + 11, mq * 256:(mq + 1) * 256], 11, 256)
            for u in range(2):
                add_unit(wpg_d[l, :, :, u * 512:(u + 1) * 512], 8, 512)
            add_unit(wpp_d[l, :, :, :], 2, 1024)
        wstate = {"issued": 0, "next": 0}

        def w_issue_upto(idx):
            while wstate["issued"] <= idx and wstate["issued"] < len(wunits):
                i = wstate["issued"]
                ap3, nk, ncol = wunits[i]
                slot = i % 3
                dst = ring[:, slot, 0:nk * ncol].rearrange("p (k c) -> p k c", c=ncol)
                cx.dma("pool", dst, ap3, writes=[cx.t("ring", slot)])
                wstate["issued"] += 1

        def w_next(nk, ncol, ahead=2):
            i = wstate["next"]
            assert wunits[i][1] == nk and wunits[i][2] == ncol, (i, wunits[i][1:], nk, ncol)
            w_issue_upto(i + ahead)
            wstate["next"] += 1
            slot = i % 3
            return ring[:, slot, 0:nk * ncol].rearrange("p (k c) -> p k c", c=ncol), cx.t("ring", slot)

        cx.dma("sp", cst[:], consts_d[:, :], writes=[cx.t("cst")])
        cx.dma("sp", gains[:], gains_d[:, :], writes=[cx.t("gains")])
        cx.dma("sp", lbraw[:].rearrange("p h l -> p (h l)"), lb_d[:, :], writes=[cx.t("lbraw")])
        cx.dma("sp", esink[:], bass.AP(sink_d, 0, [[0, 128], [1, DEPTH * 8]]), writes=[cx.t("esink")])
        cx.dma("sp", convw[:], cw_d[:, :], writes=[cx.t("convw")])
        cx.dma("sp", convb[:], cb_d[:, :], writes=[cx.t("convb")])
        cx.dma("sp", sel[:], sel_d[:, :], writes=[cx.t("sel")])
        w_issue_upto(1)
        cp(identb[:], identf, [cx.t("cst")], [cx.t("identb")])
        cp(permb[:], cst[:, C_PM:C_PM + 128], [cx.t("cst")], [cx.t("permb")])
        cp(maskb[:].rearrange("p a b -> p (a b)"), cst[:, C_MB:C_MB + 256], [cx.t("cst")], [cx.t("maskb")])
        memset(onesf[:], 1.0, [cx.t("onesf")])
        memset(onesb[:], 1.0, [cx.t("onesb")])
        act(esink[:], esink[:], AF.Exp, [cx.t("esink")], [cx.t("esink")])
        act(lbraw[:], lbraw[:], AF.Exp, [cx.t("lbraw")], [cx.t("lbraw")])
        ssum = tmp[0][:, 0:2]
        cx.op("dve", lambda g: g.reduce_sum(out=ssum, in_=lbraw[:], axis=AX.X), reads=[cx.t("lbraw")], writes=[cx.t("tmp", 0)])
        cx.op("dve", lambda g: g.reciprocal(out=tmp[0][:, 2:4], in_=ssum), reads=[cx.t("tmp", 0)], writes=[cx.t("tmp", 0)])
        tt(lbraw[:], lbraw[:], tmp[0][:, 2:4].unsqueeze(2).to_broadcast([128, 2, DEPTH]), ALU.mult,
           [cx.t("lbraw"), cx.t("tmp", 0)], [cx.t("lbraw")])
        memset(lbv[:, :, 0:1], 0.0, [cx.t("lbv")])
        for i in range(1, DEPTH):
            tt(lbv[:, :, i:i + 1], lbv[:, :, i - 1:i], lbraw[:, :, i:i + 1], ALU.add, [cx.t("lbraw"), cx.t("lbv")], [cx.t("lbv")])
        ts(oml[:], lbv[:], -1.0, 1.0, ALU.mult, ALU.add, [cx.t("lbv")], [cx.t("oml")])
        for hp in range(2):
            ts(lnm[:, 2 + hp, :], cst[:, C_SEG + 1:C_SEG + 33], cst[:, C_LNC + hp:C_LNC + hp + 1], None, ALU.mult, None,
               [cx.t("cst")], [cx.t("lnm")])
            ts(lnd[:, 2 + hp, :], cst[:, C_SEG + 1:C_SEG + 33], cst[:, C_LNC + hp:C_LNC + hp + 1], None, ALU.mult, None,
               [cx.t("cst")], [cx.t("lnd")])
            ts(mtab[:, 2 + hp, :], cst[:, C_SEG + 1:C_SEG + 33], cst[:, C_EC + hp:C_EC + hp + 1], None, ALU.mult, None,
               [cx.t("cst")], [cx.t("mtab")])
            ts(dtab[:, 2 + hp, :], cst[:, C_SEG + 1:C_SEG + 33], cst[:, C_EC + hp:C_EC + hp + 1], None, ALU.mult, None,
               [cx.t("cst")], [cx.t("dtab")])

        with ExitStack() as ps0:
            posi = sb("posi", [128, T], I32, ps0)
            th = sb("th", [128, T], F32, ps0)
            kf = sb("kf", [128, T], F32, ps0)
            rr = sb("rr", [128, T], F32, ps0)
            tpos = cx.t("postmp")
            C1 = 6.28125
            C2 = 2.0 * math.pi - 6.28125
            tsteps = []
            tsteps.append(lambda: cx.dma("sp", posi[:], bass.AP(pos_d, 0, [[0, 128], [1, T]]), writes=[tpos]))
            tsteps.append(lambda: cp(th[:], posi[:], [tpos], [tpos]))
            tsteps.append(lambda: ts(th[:], th[:], cst[:, C_INV:C_INV + 1], None, ALU.mult, None, [tpos, cx.t("cst")], [tpos]))
            tsteps.append(lambda: ts(posi[:], th[:], 1.0 / (2.0 * math.pi), None, ALU.mult, None, [tpos], [tpos]))
            tsteps.append(lambda: cp(kf[:], posi[:], [tpos], [tpos]))
            tsteps.append(lambda: stt(rr[:], kf[:], -C1, th[:], ALU.mult, ALU.add, [tpos], [tpos]))
            tsteps.append(lambda: stt(rr[:], kf[:], -C2, rr[:], ALU.mult, ALU.add, [tpos], [tpos]))
            tsteps.append(lambda: ts(kf[:], rr[:], math.pi, None, ALU.is_gt, None, [tpos], [tpos]))
            tsteps.append(lambda: stt(rr[:], kf[:], -2.0 * math.pi, rr[:], ALU.mult, ALU.add, [tpos], [tpos]))
            tsteps.append(lambda: ts(kf[:], rr[:], -math.pi, None, ALU.is_lt, None, [tpos], [tpos]))
            tsteps.append(lambda: stt(rr[:], kf[:], 2.0 * math.pi, rr[:], ALU.mult, ALU.add, [tpos], [tpos]))
            tsteps.append(lambda: act(th[:], rr[:], AF.Sin, [tpos, cx.t("cst")], [tpos], scale=cst[:, C_SIGN:C_SIGN + 1]))
            tsteps.append(lambda: cx.dma("act", tab_s[1, :, :], th[:], reads=[tpos], writes=[cx.t("tab_s")]))
            tsteps.append(lambda: ts(rr[:], rr[:], math.pi / 2.0, None, ALU.add, None, [tpos], [tpos]))
            tsteps.append(lambda: ts(kf[:], rr[:], math.pi, None, ALU.is_gt, None, [tpos], [tpos]))
            tsteps.append(lambda: stt(rr[:], kf[:], -2.0 * math.pi, rr[:], ALU.mult, ALU.add, [tpos], [tpos]))
            tsteps.append(lambda: act(kf[:], rr[:], AF.Sin, [tpos], [tpos]))
            tsteps.append(lambda: cx.dma("act", tab_s[0, :, :], kf[:], reads=[tpos], writes=[cx.t("tab_s")]))
            xb = [sb("xb%d" % i, [128, D], F32, ps0) for i in range(2)]
            for j in range(NT):
                s = j % 2
                cx.dma("sp", xb[s][:], x_d[j * 128:(j + 1) * 128, :], writes=[cx.t("xb", s)])
                for half in range(2):
                    b, bt = bk(half + 2 * (j % 2))
                    for q in range(4):
                        m = half * 4 + q
                        mm1(b[:, q * 128:(q + 1) * 128], xb[s][:, m * 128:(m + 1) * 128], identf, [cx.t("xb", s), cx.t("cst")],
                            [bt], signal=(q == 3), tr=True)
                    cp(rT[:, half * 4:half * 4 + 4, j * 128:(j + 1) * 128], b[:].rearrange("p (q t) -> p q t", t=128), [bt],
                       [cx.t("rT", half * 4 + q, j // 4) for q in range(4)], e=("act" if half else "dve"))
                for _ in range(2):
                    if tsteps:
                        tsteps.pop(0)()
            while tsteps:
                tsteps.pop(0)()
            cx.barrier()

        def rT_trks(tb):
            return [cx.t("rT", k, tb) for k in range(8)]

        def hm_trks(tb):
            return [cx.t("hm", k, tb) for k in range(8)]

        epsc = sb("epsc", [128, 1], F32)
        sq2 = sb("sq2", [128, 8, 2], BF)
        rs2 = sb("rs2", [128, 2], F32)
        xs2 = sb("xs2", [128, 16], F32)
        memset(epsc[:], EPS, [cx.t("epsc")])

        sq_i = [0]

        def rstd_block2(tb, bank_i):
            b, bt = bk(bank_i)
            for kk in range(4):
                si = sq_i[0] % 2
                sq_i[0] += 1
                act(sqb[si][:], rT[:, 2 * kk:2 * kk + 2, tb * 512:(tb + 1) * 512], AF.Square,
                    [cx.t("rT", 2 * kk, tb), cx.t("rT", 2 * kk + 1, tb)], [cx.t("sqb", si)])
                for i2 in range(2):
                    k = 2 * kk + i2
                    mm1(b[:], onesb[:], sqb[si][:, i2, :], [cx.t("sqb", si), cx.t("onesb")], [bt] if (k == 0 or k == 7) else (),
                        start=(k == 0), stop=(k == 7), signal=(i2 == 1))
            rs, rst = nexttmp()
            act(rs[:], b[:], AF.Ln, [bt, cx.t("epsc")], [rst], scale=1.0 / D, bias=epsc[:, 0:1])
            act(rs[:], rs[:], AF.Exp, [rst], [rst], scale=-0.5)
            return rs, rst

        def norm_to_hm(goff):
            for tb in range(NB):
                rs, rst = rstd_block2(tb, tb % 2)
                for k in range(8):
                    stt(hmT[:, k, 2 + tb * 512:2 + (tb + 1) * 512], rT[:, k, tb * 512:(tb + 1) * 512], gains[:, goff + k:goff + k + 1],
                        rs[:], ALU.mult, ALU.mult, [cx.t("rT", k, tb), rst, cx.t("gains")], [cx.t("hm", k, tb)])

        def dump(stage_l, stage_s):
            if stop is not None and stop == (stage_l, stage_s):
                cx.barrier()
                cx.dma("sp", dbg_d[:, :, :], rT[:], reads=[cx.t("rT", k, tb) for k in range(8) for tb in range(NB)],
                       writes=[cx.t("dbg")])
                return True
            return False

        stopped = False
        for l in range(n_layers):
            if stopped:
                break
            G_ATTN = (l * 3 + 0) * 8
            G_FFN = (l * 3 + 1) * 8
            G_PLE = (l * 3 + 2) * 8
            with ExitStack() as ph:
                kTa = sb("kTa", [128, 128 + T], BF, ph)
                vaug = sb("vaug", [128, NT + 1, 2, 65], BF, ph)
                gnl = sb("gnl", [128, 256], F32, ph)
                cx.dma("sp", gnl[:], bass.AP(gn_d, l * 256, [[0, 128], [1, 256]]), writes=[cx.t("gnl")])
                memset(vaug[:, 1:, :, 64:65], 1.0, [cx.t("vaug_ones")])
                norm_to_hm(G_ATTN)

                with ExitStack() as ph1:
                    cosT = sb("cosT", [128, T], F32, ph1)
                    ssT = sb("ssT", [128, T], F32, ph1)
                    blkb = [sb("blkb%d" % i, [128, 512], BF, ph1) for i in range(6)]
                    bi = [0]

                    def nextblk():
                        i = bi[0] % 6
                        bi[0] += 1
                        return blkb[i], cx.t("blkb", i)

                    cx.dma("sp", cosT[:], tab_s[0, :, :], reads=[cx.t("tab_s")], writes=[cx.t("cosT")])
                    cx.dma("sp", ssT[:], tab_s[1, :, :], reads=[cx.t("tab_s")], writes=[cx.t("ssT")])
                    w, wt = w_next(8, 512)
                    for tb in range(NB):
                        bs = [bk(4 * (tb % 2) + q) for q in range(4)]
                        hsl = slice(2 + tb * 512, 2 + (tb + 1) * 512)
                        blk = slice(tb * 512, (tb + 1) * 512)
                        for q in range(4):
                            mm_group(bs[q][0][:], [(w[:, k, q * 128:(q + 1) * 128], hmT[:, k, hsl]) for k in range(8)],
                                     hm_trks(tb) + [wt], bs[q][1])
                        for hp in range(2):
                            (qb, qbt), (fb, fbt) = bs[2 * hp], bs[2 * hp + 1]
                            sig, sigt = nexttmp()
                            km, kmt = nexttmp()
                            lf, lft = nexttmp()
                            act(sig[:], fb[:], AF.Sigmoid, [fbt], [sigt])
                            ts(km[:], sig[:], -1.0, 1.0, ALU.mult, ALU.add, [sigt], [kmt], e="pool")
                            act(lf[:], sig[:], AF.Ln, [sigt, cx.t("oml"), cx.t("lbv")], [lft], scale=oml[:, hp, l:l + 1],
                                bias=lbv[:, hp, l:l + 1])
                            B, Bt = sig, sigt
                            cx.op("dve", lambda g, B=B, lf=lf: g.tensor_tensor_scan(out=B[:], data0=cst[:, C_SEG:C_SEG + 512], data1=lf[:],
                                                                                  initial=0.0, op0=ALU.mult, op1=ALU.add),
                                  reads=[lft, cx.t("cst")], writes=[Bt])
                            Bv = B[:].rearrange("p (n k) -> p n k", k=64)
                            cp(lnm[:, hp, tb * 8:(tb + 1) * 8], Bv[:, :, 31], [Bt], [cx.t("lnm")])
                            act(mtab[:, hp, tb * 8:(tb + 1) * 8], Bv[:, :, 31], AF.Exp, [Bt], [cx.t("mtab")])
                            Dt_, Dtt = lf, lft
                            tt(Dt_[:].rearrange("p (n k) -> p n k", k=64), Bv, Bv[:, :, 31:32].to_broadcast([128, 8, 64]),
                               ALU.subtract, [Bt], [Dtt])
                            Dv = Dt_[:].rearrange("p (n k) -> p n k", k=64)
                            cp(lnd[:, hp, tb * 8:(tb + 1) * 8], Dv[:, :, 63], [Dtt], [cx.t("lnd")])
                            EA, EAt = nexttmp()
                            EB, EBt = sig, sigt
                            act(EA[:], Dt_[:], AF.Exp, [Dtt], [EAt])
                            act(EB[:], Dt_[:], AF.Exp, [Dtt], [EBt], scale=-1.0)
                            cp(dtab[:, hp, tb * 8:(tb + 1) * 8], EA[:].rearrange("p (n k) -> p n k", k=64)[:, :, 63], [EAt],
                               [cx.t("dtab")])
                            ab, abt = nextblk()
                            tt(ab[:], qb[:], EA[:], ALU.mult, [qbt, EAt], [abt])
                            cx.dma("sp", a_s[hp, :, blk], ab[:], reads=[abt], writes=[cx.t("a_s", tb)])
                            bb2, bbt2 = nextblk()
                            stt(bb2[:], km[:], oml[:, hp, l:l + 1], EB[:], ALU.mult, ALU.mult, [kmt, EBt, cx.t("oml")], [bbt2])
                            cx.dma("sp", b_s[hp, :, blk], bb2[:], reads=[bbt2], writes=[cx.t("b_s", tb)])

                    xsb = [sb("xsb%d" % i, [128, 512], BF, ph1) for i in range(4)]
                    hcount = [0]

                    def rope_unit(w, wt, nch, finish):
                        halves = []
                        for tb in range(NB):
                            for h0 in range(0, nch, 2):
                                halves.append((tb, list(range(h0, min(h0 + 2, nch)))))
                        state = {}

                        def G(hi):
                            tb, cis = halves[hi]
                            st = hcount[0] % 2
                            hcount[0] += 1
                            state[hi] = st
                            hsl = slice(2 + tb * 512, 2 + (tb + 1) * 512)
                            for n_, ci in enumerate(cis):
                                xb_, xbt = bk(4 * st + n_)
                                mm_group(xb_[:], [(w[:, k, ci * 128:(ci + 1) * 128], hmT[:, k, hsl]) for k in range(8)], hm_trks(tb) + [wt], xbt)
                                cp(xsb[2 * st + n_][:], xb_[:], [xbt], [cx.t("xsb", 2 * st + n_)], e="act")

                        def Pm(hi):
                            tb, cis = halves[hi]
                            st = state[hi]
                            blk = slice(tb * 512, (tb + 1) * 512)
                            for n_, ci in enumerate(cis):
                                xb_, xbt = bk(4 * st + n_)
                                xs_, xst = bk(4 * st + 2 + n_)
                                mm1(xs_[:], permb[:], xsb[2 * st + n_][:], [cx.t("permb"), cx.t("xsb", 2 * st + n_)], [xst])
                                t1, t1t = nexttmp()
                                t2, t2t = nexttmp()
                                tt(t1[:], xb_[:], cosT[:, blk], ALU.mult, [xbt, cx.t("cosT")], [t1t])
                                tt(t2[:], xs_[:], ssT[:, blk], ALU.mult, [xst, cx.t("ssT")], [t2t])
                                finish(ci, tb, t1, t1t, t2, t2t)

                        for hi in range(len(halves) + 1):
                            if hi < len(halves):
                                G(hi)
                            if hi >= 1:
                                Pm(hi - 1)

                    def fin_c(ci, tb, t1, t1t, t2, t2t):
                        hp, which = ci // 2, ci % 2
                        blk = slice(tb * 512, (tb + 1) * 512)
                        tt(t1[:], t1[:], t2[:], ALU.add, [t1t, t2t], [t1t], e="pool")
                        cbase = (C_EBC if which else C_EAC) + hp * 64
                        ctab = cst[:, cbase:cbase + 64]
                        ob_, obt_ = nextblk()
                        tt(ob_[:].rearrange("p (n k) -> p n k", k=64), t1[:].rearrange("p (n k) -> p n k", k=64),
                           ctab.unsqueeze(1).to_broadcast([128, 8, 64]), ALU.mult, [t1t, cx.t("cst")], [obt_], e="pool")
                        dst = (b_s if which else a_s)
                        cx.dma("sp", dst[2 + hp, :, blk], ob_[:], reads=[obt_], writes=[cx.t("b_s" if which else "a_s", tb)])

                    def fin_q(ci, tb, t1, t1t, t2, t2t):
                        blk = slice(tb * 512, (tb + 1) * 512)
                        ob_, obt_ = nextblk()
                        tt(ob_[:], t1[:], t2[:], ALU.add, [t1t, t2t], [obt_], e="pool")
                        cx.dma("sp", q_s[ci, :, blk], ob_[:], reads=[obt_], writes=[cx.t("q_s", tb)])

                    def fin_k(ci, tb, t1, t1t, t2, t2t):
                        tt(kTa[:, 128 + tb * 512:128 + (tb + 1) * 512], t1[:], t2[:], ALU.add, [t1t, t2t], [cx.t("kTa", tb)])

                    w, wt = w_next(8, 512)
                    rope_unit(w, wt, 4, fin_c)
                    w, wt = w_next(8, 512)
                    rope_unit(w, wt, 4, fin_q)
                    w, wt = w_next(8, 128)
                    rope_unit(w, wt, 1, fin_k)
                    cx.barrier()

                wv, wvt = w_next(8, 512, 2)
                wg, wgt = w_next(8, 512, 1)
                wbv, wbvt = w_next(8, 128, 0)
                with ExitStack() as ph2:
                    v_all = sb("v_all", [128, NT, 512], BF, ph2)
                    a_t = [sb("a_t%d" % i, [128, 4, 128], BF, ph2) for i in range(6)]
                    bt_t = [sb("bt_t%d" % i, [128, 4, 128], BF, ph2) for i in range(6)]
                    b_t = [sb("b_t%d" % i, [128, 512], BF, ph2) for i in range(2)]
                    g_t = [sb("g_t%d" % i, [128, 512], BF, ph2) for i in range(2)]
                    o_t = [sb("o_t%d" % i, [128, 512], F32, ph2) for i in range(2)]
                    scm = [sb("scm%d" % i, [128, 4, 2, 64], BF, ph2) for i in range(2)]
                    sp_t = [sb("sp_t%d" % i, [128, 2, 4, 64], BF, ph2) for i in range(2)]
                    Spf = sb("Spf", [128, 4, 64], F32, ph2)
                    U = sb("U", [128, 4, 64], F32, ph2)
                    Sfull = sb("Sfull", [128, 4, 64], F32, ph2)
                    xs = sb("xs", [128, XW], F32, ph2)
                    memset(Spf[:], 0.0, [cx.t("Spf")])
                    for j in range(NT):
                        s = j % 2
                        tb = j // 4
                        hcol = slice(2 + j * 128, 2 + (j + 1) * 128)
                        bv_, bvt = bk(3 * (j % 2))
                        mm_group(bv_[:], [(hmT[:, k, hcol], wv[:, k, :]) for k in range(8)], hm_trks(tb) + [wvt], bvt)
                        cp(v_all[:, j, :], bv_[:], [bvt], [cx.t("v_all", j)], e="dve")
                        bg_, bgt = bk(3 * (j % 2) + 1)
                        mm_group(bg_[:], [(hmT[:, k, hcol], wg[:, k, :]) for k in range(8)], hm_trks(tb) + [wgt], bgt)
                        act(g_t[s][:], bg_[:], AF.Silu, [bgt], [cx.t("g_t", s)])
                        tt(g_t[s][:, 0:256], g_t[s][:, 0:256], gnl[:], ALU.mult, [cx.t("g_t", s), cx.t("gnl")], [cx.t("g_t", s)], e="pool")
                        cx.dma("sp", g_s[j, :, :], g_t[s][:], reads=[cx.t("g_t", s)], writes=[cx.t("g_s", j)])
                        bw_, bwt = bk(3 * (j % 2) + 2)
                        mm_group(bw_[:, 0:128], [(hmT[:, k, hcol], wbv[:, k, :]) for k in range(8)], hm_trks(tb) + [wbvt], bwt)
                        cp(vaug[:, 1 + j, :, 0:64], bw_[:, 0:128].rearrange("p (a b) -> p a b", b=64), [bwt], [cx.t("vaug", 1 + j)], e="act")


                    def load_ab(j):
                        s_ = j % 6
                        tb_ = j // 4
                        cx.dma("sp", a_t[s_][:], a_s[:, :, j * 128:(j + 1) * 128].rearrange("h p t -> p h t"), reads=[cx.t("a_s", tb_)],
                               writes=[cx.t("a_t", s_)])
                        cx.dma("sp", bt_t[s_][:], b_s[:, :, j * 128:(j + 1) * 128].rearrange("h p t -> p h t"), reads=[cx.t("b_s", tb_)],
                               writes=[cx.t("bt_t", s_)])

                    tt(ctab[:, :, 0:31], dtab[:, :, 0:31], mtab[:, :, 1:32], ALU.mult, [cx.t("dtab"), cx.t("mtab")], [cx.t("ctab")])
                    tt(cstab[:], lnm[:], lnd[:], ALU.add, [cx.t("lnm"), cx.t("lnd")], [cx.t("cstab")])
                    for hp in range(4):
                        cx.op("dve", lambda g, hp=hp: g.tensor_tensor_scan(out=etab[:, hp, :], data0=cst[:, C_SEG + 1:C_SEG + 33],
                                                                        data1=cstab[:, hp, :], initial=0.0, op0=ALU.mult, op1=ALU.add),
                              reads=[cx.t("cstab"), cx.t("cst")], writes=[cx.t("etab")])
                    act(dtot[:], etab[:, :, 31], AF.Exp, [cx.t("etab")], [cx.t("dtot")])
                    tt(cstab[:], etab[:], lnd[:], ALU.subtract, [cx.t("etab"), cx.t("lnd")], [cx.t("cstab")])
                    act(etab[:], cstab[:], AF.Exp, [cx.t("cstab")], [cx.t("etab")])

                    def f0(j):
                        s = j % 2
                        s4 = j % 6
                        bb_, bbt = bk(s)
                        for hp in range(4):
                            mm1(bb_[:, hp * 128:(hp + 1) * 128], bt_t[s4][:, hp, :], identb[:], [cx.t("bt_t", s4), cx.t("identb")],
                                [bbt], signal=(hp == 3))

                    def f1(j):
                        s = j % 2
                        bb_, bbt = bk(s)
                        cp(b_t[s][:], bb_[:], [bbt], [cx.t("b_t", s)], e="act")

                    def f2(j):
                        s = j % 2
                        s4 = j % 6
                        Xb = [bk(2 + 2 * s), bk(3 + 2 * s)]
                        for e2 in range(2):
                            er = slice(e2 * 64, (e2 + 1) * 64)
                            for ch in range(2):
                                cc = slice(ch * 64, (ch + 1) * 64)
                                for hp in range(4):
                                    mm1(Xb[e2][0][ch * 64:(ch + 1) * 64, hp * 64:(hp + 1) * 64], bt_t[s4][er, hp, cc], a_t[s4][er, hp, cc],
                                        [cx.t("bt_t", s4), cx.t("a_t", s4)], [Xb[e2][1]], signal=(ch == 1 and hp == 3))
                        for ch in range(2):
                            rows = slice(ch * 64, (ch + 1) * 64)
                            for h8 in range(8):
                                hp, e2 = h8 // 2, h8 % 2
                                mm1(Xb[ch][0][e2 * 64:(e2 + 1) * 64, 256 + hp * 64:256 + (hp + 1) * 64], b_t[s][rows, h8 * 64:(h8 + 1) * 64],
                                    v_all[rows, j, h8 * 64:(h8 + 1) * 64], [cx.t("b_t", s), cx.t("v_all", j)], [Xb[ch][1]],
                                    signal=(h8 == 7))

                    def f3(j):
                        s = j % 2
                        Xb = [bk(2 + 2 * s), bk(3 + 2 * s)]
                        for e2 in range(2):
                            tt(scm[s][:, :, e2, :], Xb[e2][0][:, 0:256].rearrange("p (h t) -> p h t", t=64),
                               caus.unsqueeze(1).to_broadcast([128, 4, 64]), ALU.mult, [Xb[e2][1], cx.t("cst")], [cx.t("scm", s, e2)])
                        cp(sp_t[s][:, 0, :, :], Spf[:], [cx.t("Spf")], [cx.t("sp_t", s)], e="act")
                        for ch in range(2):
                            n = 2 * j + ch
                            tt(U[:], Spf[:], Xb[ch][0][:, 256:512].rearrange("p (h c) -> p h c", c=64), ALU.add, [cx.t("Spf"), Xb[ch][1]],
                               [cx.t("U")])
                            if n < 31:
                                tt(Spf[:], U[:], ctab[:, :, n:n + 1].to_broadcast([128, 4, 64]), ALU.mult, [cx.t("U"), cx.t("ctab")],
                                   [cx.t("Spf")])
                                if ch == 0:
                                    cp(sp_t[s][:, 1, :, :], Spf[:], [cx.t("Spf")], [cx.t("sp_t", s)], e="act")
                            else:
                                tt(Sfull[:], U[:], dtab[:, :, n:n + 1].to_broadcast([128, 4, 64]), ALU.mult, [cx.t("U"), cx.t("dtab")],
                                   [cx.t("Sfull")])
                        cx.dma("sp", sp_s[j, :, :], sp_t[s][:].rearrange("p a h c -> p (a h c)"), reads=[cx.t("sp_t", s)], writes=[cx.t("sp_s", j)])

                    def f4(j):
                        s = j % 2
                        Rb = [bk(6), bk(7)]
                        for ch in range(2):
                            rows = slice(ch * 64, (ch + 1) * 64)
                            for h8 in range(8):
                                hp, e2 = h8 // 2, h8 % 2
                                mm1(Rb[ch][0][rows, h8 * 64:(h8 + 1) * 64], scm[s][rows, hp, e2, :], v_all[rows, j, h8 * 64:(h8 + 1) * 64],
                                    [cx.t("scm", s, e2), cx.t("v_all", j)], [Rb[ch][1]], signal=(h8 == 7))

                    def f5(j):
                        s = j % 2
                        Rb = [bk(6), bk(7)]
                        cp(o_t[s][0:64, :], Rb[0][0][0:64, :], [Rb[0][1]], [cx.t("o_t", s)], e="act")
                        cp(o_t[s][64:128, :], Rb[1][0][64:128, :], [Rb[1][1]], [cx.t("o_t", s)], e="act")
                        cx.dma("sp", o_s[j, :, :], o_t[s][:], reads=[cx.t("o_t", s)], writes=[cx.t("o_s", j)])

                    def f_half(j):
                        pass

                    pipeline([f0, f1, f2, f3, f4, f5, f_half], list(range(NT)), loads=load_ab, lead=4)

                    cp(xs[:, 0:256], Sfull[:].rearrange("p h c -> p (h c)"), [cx.t("Sfull")], [cx.t("xs")])
                    cp(xs[:, 256:260], dtot[:], [cx.t("dtot")], [cx.t("xs")])
                    cp(xs[:, 260:388], kTa[:, 128 + 15 * 128:128 + 16 * 128], [cx.t("kTa", 3)], [cx.t("xs")])
                    cp(xs[:, 388:518], vaug[:, NT, :, :].rearrange("p a b -> p (a b)"), [cx.t("vaug", NT), cx.t("vaug_ones")], [cx.t("xs")])
                    memset(xs[:, 518:520], 0.0, [cx.t("xs")])
                    cx.dma("sp", x1s[l][:, :], xs[:], reads=[cx.t("xs")], writes=[cx.t("x1s", l)])
                    cx.collective(x1s[l].ap().opt(), x1g[l].ap().opt(), GROUPS, reads=[cx.t("x1s", l)], writes=[cx.t("x1g", l)])
                    cx.barrier()

                with ExitStack() as ph3:
                    S_in = sb("S_in", [128, 4, 64], F32, ph3)
                    with ExitStack() as ph3a:
                        kp = [sb("kp%d" % i, [128, 2, 2, 128], BF, ph3a) for i in range(2)]
                        q_t = [sb("q_t%d" % i, [128, 4, 128], BF, ph3a) for i in range(7)]
                        xg = sb("xg", [128, 4, XW], F32, ph3a)
                        X = sb("X", [128, 4, 64], F32, ph3a)
                        PT = [sb("PT%d" % i, [128, 512], BF, ph3a) for i in range(8)]
                        den = sb("den", [128, 8], F32, ph3a)
                        osw = sb("osw", [128, 8, 64], BF, ph3a)
                        memset(kp[0][:], 0.0, [cx.t("kp", 0)])
                        memset(kp[1][:], 0.0, [cx.t("kp", 1)])

                        def load_q(j, slot):
                            cx.dma("sp", q_t[slot][:], q_s[:, :, j * 128:(j + 1) * 128].rearrange("c p t -> p c t"), reads=[cx.t("q_s", j // 4)],
                                   writes=[cx.t("q_t", slot)])

                        def swa_kp(j, slot):
                            tb = j // 4
                            pp = j % 2
                            kpt = cx.t("kp", pp)
                            ktr = [cx.t("kTa", tb)] + ([cx.t("kTa", tb - 1)] if (j % 4 == 0 and j > 0) else []) + ([cx.t("kTa_halo")] if j == 0 else [])
                            for kv in range(2):
                                r = slice(kv * 64, (kv + 1) * 64)
                                cp(kp[pp][r, kv, :, :].rearrange("p a b -> p (a b)"), kTa[r, j * 128:j * 128 + 256], ktr, [kpt], e="dve")

                        def swa_qk(j, slot):
                            pp = j % 2
                            kpt = cx.t("kp", pp)
                            SW = [bk(1), bk(2), bk(3), bk(4)]
                            qtr = [cx.t("q_t", slot)]
                            for kv in range(2):
                                for kb in range(2):
                                    b_, bt_ = SW[kv * 2 + kb]
                                    mm1(b_[:], kp[pp][:, kv, kb, :], q_t[slot][:], [kpt] + qtr, [bt_], start=True, stop=False, signal=False)
                                    mm1(b_[:], identb[:], maskb[:, kb, :].unsqueeze(1).to_broadcast([128, 4, 128]),
                                        [cx.t("identb"), cx.t("maskb")], [bt_], start=False, stop=True)

                        def swa_exp(j, slot):
                            pp = j % 2
                            SW = [bk(1), bk(2), bk(3), bk(4)]
                            for kv in range(2):
                                for kb in range(2):
                                    b_, bt_ = SW[kv * 2 + kb]
                                    act(PT[pp * 4 + kv * 2 + kb][:], b_[:], AF.Exp, [bt_], [cx.t("PT", pp * 4 + kv * 2 + kb)], scale=0.125)

                        def swa_mask(j):
                            pp = j % 2
                            for kv in range(2):
                                for kb in range(2):
                                    pi = pp * 4 + kv * 2 + kb
                                    tt(PT[pi][:].rearrange("p (c q) -> p c q", q=128), PT[pi][:].rearrange("p (c q) -> p c q", q=128),
                                       maskb[:, kb, :].unsqueeze(1).to_broadcast([128, 4, 128]), ALU.mult, [cx.t("PT", pi), cx.t("maskb")],
                                       [cx.t("PT", pi)])

                        def swa_pv(j):
                            pp = j % 2
                            OB = [bk(5), bk(6)]
                            vtr = [cx.t("vaug", j), cx.t("vaug", j + 1), cx.t("vaug_ones")]
                            for h in range(8):
                                kv, c = h // 4, h % 4
                                ob, obt = OB[h // 4]
                                oo = ob[:, (h % 4) * 65:(h % 4) * 65 + 65]
                                p0, p1 = pp * 4 + kv * 2 + 0, pp * 4 + kv * 2 + 1
                                mm1(oo, PT[p0][:, c * 128:(c + 1) * 128], vaug[:, j, kv, :], [cx.t("PT", p0)] + vtr, [obt],
                                    start=True, stop=False, signal=False)
                                mm1(oo, PT[p1][:, c * 128:(c + 1) * 128], vaug[:, j + 1, kv, :], [cx.t("PT", p1)] + vtr, [obt],
                                    start=False, stop=True, signal=(h % 4 == 3))

                        def swa_norm(j):
                            OB = [bk(5), bk(6)]
                            for q in range(2):
                                obv = OB[q][0][:, 0:260].rearrange("p (h c) -> p h c", c=65)
                                tt(den[:, q * 4:(q + 1) * 4], obv[:, :, 64], esink[:, l * 8 + q * 4:l * 8 + (q + 1) * 4], ALU.add,
                                   [OB[q][1], cx.t("esink")], [cx.t("den")])
                            cx.op("dve", lambda g: g.reciprocal(out=den[:], in_=den[:]), reads=[cx.t("den")], writes=[cx.t("den")])
                            for q in range(2):
                                obv = OB[q][0][:, 0:260].rearrange("p (h c) -> p h c", c=65)
                                tt(osw[:, q * 4:(q + 1) * 4, :], obv[:, :, 0:64], den[:, q * 4:(q + 1) * 4].unsqueeze(2).to_broadcast([128, 4, 64]),
                                   ALU.mult, [OB[q][1], cx.t("den")], [cx.t("osw")])

                        def swa_tr(j):
                            tr, trt = bk(7)
                            for q in range(4):
                                mm1(tr[:, q * 128:(q + 1) * 128], osw[:, 2 * q:2 * q + 2, :].rearrange("p a b -> p (a b)"), identb[:],
                                    [cx.t("osw"), cx.t("identb")], [trt], signal=(q == 3))

                        def swa_out(j):
                            tb = j // 4
                            tr, trt = bk(7)
                            cp(hmT[:, 2:6, 2 + j * 128:2 + (j + 1) * 128], tr[:].rearrange("p (q t) -> p q t", t=128), [trt],
                               [cx.t("hm", k, tb) for k in range(2, 6)], e="dve")

                        def recv_x1():
                            cx.dma("sp", xg[:], x1g[l].ap().rearrange("(r p) f -> p r f", p=128), reads=[cx.t("x1g", l)], writes=[cx.t("xg")])
                            xgt = cx.t("xg")
                            Sl = lambda r: xg[:, r, 0:256].rearrange("p (h c) -> p h c", c=64)
                            Dl = lambda r: xg[:, r, 256:260].unsqueeze(2).to_broadcast([128, 4, 64])
                            selt = cx.t("sel")
                            Sf = S_in[:].rearrange("p h c -> p (h c)")
                            Xf = X[:].rearrange("p h c -> p (h c)")
                            ts(Sf, xg[:, 0, 0:256], sel[:, 1:2], None, ALU.mult, None, [xgt, selt], [cx.t("S_in")])
                            tt(X[:], Sl(0), Dl(1), ALU.mult, [xgt], [cx.t("X")])
                            tt(X[:], X[:], Sl(1), ALU.add, [xgt, cx.t("X")], [cx.t("X")])
                            stt(Sf, Xf, sel[:, 2:3], Sf, ALU.mult, ALU.add, [cx.t("X"), cx.t("S_in"), selt], [cx.t("S_in")])
                            tt(X[:], X[:], Dl(2), ALU.mult, [xgt, cx.t("X")], [cx.t("X")])
                            tt(X[:], X[:], Sl(2), ALU.add, [xgt, cx.t("X")], [cx.t("X")])
                            stt(Sf, Xf, sel[:, 3:4], Sf, ALU.mult, ALU.add, [cx.t("X"), cx.t("S_in"), selt], [cx.t("S_in")])
                            hk, hkt = nexttmp()
                            ts(hk[:, 0:258], xg[:, 0, 260:518], sel[:, 4:5], None, ALU.mult, None, [xgt, selt], [hkt])
                            for r in range(1, 4):
                                stt(hk[:, 0:258], xg[:, r, 260:518], sel[:, 4 + r:5 + r], hk[:, 0:258], ALU.mult, ALU.add, [xgt, selt, hkt], [hkt])
                            cp(kTa[:, 0:128], hk[:, 0:128], [hkt], [cx.t("kTa_halo")])
                            cp(vaug[:, 0, :, :].rearrange("p a b -> p (a b)"), hk[:, 128:258], [hkt], [cx.t("vaug", 0)])

                        def qslot(j):
                            return 6 if j == 0 else j % 6

                        def swa_half(j):
                            if j == 3:
                                w_issue_upto(wstate["next"])
                            if j == 9:
                                w_issue_upto(wstate["next"] + 1)

                        swa_tiles = list(range(1, NT)) + [0]
                        pipeline([lambda j: swa_kp(j, qslot(j)), lambda j: swa_qk(j, qslot(j)), lambda j: swa_exp(j, qslot(j)), swa_pv, swa_norm,
                                  swa_tr, swa_out, swa_half], swa_tiles, loads=lambda j: load_q(j, qslot(j)), lead=4,
                                 hook=(len(swa_tiles) - 1, recv_x1))

                        cx.barrier()
                    a2 = [sb("a2_%d" % i, [128, 4, 128], BF, ph3) for i in range(6)]
                    sp2 = [sb("sp2_%d" % i, [128, 2, 4, 64], BF, ph3) for i in range(6)]
                    Sc = [sb("Sc%d" % i, [128, 2, 4, 64], BF, ph3) for i in range(2)]
                    o2 = [sb("o2_%d" % i, [128, 512], F32, ph3) for i in range(6)]
                    g2 = [sb("g2_%d" % i, [128, 512], BF, ph3) for i in range(4)]
                    ysq = [sb("ysq%d" % i, [128, 512], F32, ph3) for i in range(4)]
                    ssq = [sb("ssq%d" % i, [128, 8], F32, ph3) for i in range(4)]
                    mixt = [sb("mixt%d" % i, [128, 512], BF, ph3) for i in range(2)]

                    def load2(j):
                        s_ = j % 6
                        cx.dma("sp", a2[s_][:], a_s[:, :, j * 128:(j + 1) * 128].rearrange("h p t -> p h t"), reads=[cx.t("a_s", j // 4)],
                               writes=[cx.t("a2", s_)])
                        cx.dma("sp", sp2[s_][:].rearrange("p a h c -> p (a h c)"), sp_s[j, :, :], reads=[cx.t("sp_s", j)], writes=[cx.t("sp2", s_)])

                    def q0(j):
                        s = j % 2
                        tt(Sc[s][:], S_in[:].unsqueeze(1).to_broadcast([128, 2, 4, 64]),
                           etab[:, :, 2 * j:2 * j + 2].rearrange("p h n -> p n h").unsqueeze(3).to_broadcast([128, 2, 4, 64]), ALU.mult,
                           [cx.t("S_in"), cx.t("etab")], [cx.t("Sc", s)], e="pool")
                        cx.dma("sp", o2[j % 6][:], o_s[j, :, :], reads=[cx.t("o_s", j)], writes=[cx.t("o2", j % 6)])

                    def q1(j):
                        s = j % 2
                        s4 = j % 6
                        Cb = [bk(1 + 2 * s), bk(2 + 2 * s)]
                        for e2 in range(2):
                            er = slice(e2 * 64, (e2 + 1) * 64)
                            for ch in range(2):
                                cc = slice(ch * 64, (ch + 1) * 64)
                                for hp in range(4):
                                    oo = Cb[e2][0][ch * 64:(ch + 1) * 64, hp * 64:(hp + 1) * 64]
                                    mm1(oo, a2[s4][er, hp, cc], sp2[s4][er, ch, hp, :], [cx.t("a2", s4), cx.t("sp2", s4)], [Cb[e2][1]],
                                        start=True, stop=False, signal=False)
                                    mm1(oo, a2[s4][er, hp, cc], Sc[s][er, ch, hp, :], [cx.t("a2", s4), cx.t("Sc", s)], [Cb[e2][1]],
                                        start=False, stop=True, signal=(ch == 1 and hp == 3))

                    def q2(j):
                        s = j % 2
                        s6 = j % 6
                        Cb = [bk(1 + 2 * s), bk(2 + 2 * s)]
                        ov = o2[s6][:].rearrange("p (h e c) -> p h e c", e=2, c=64)
                        for e2 in range(2):
                            tt(ov[:, :, e2, :], ov[:, :, e2, :], Cb[e2][0][:, 0:256].rearrange("p (h c) -> p h c", c=64), ALU.add,
                               [cx.t("o2", s6), Cb[e2][1]], [cx.t("o2", s6)])

                    def q3(j):
                        act(ysq[j % 4][:], o2[j % 6][:], AF.Square, [cx.t("o2", j % 6)], [cx.t("ysq", j % 4)])
                        cx.dma("sp", g2[j % 4][:], g_s[j, :, :], reads=[cx.t("g_s", j)], writes=[cx.t("g2", j % 4)])

                    def q4(j):
                        y4 = j % 4
                        cx.op("dve", lambda g: g.reduce_sum(out=ssq[y4][:], in_=ysq[y4][:].rearrange("p (h c) -> p h c", c=64), axis=AX.X),
                              reads=[cx.t("ysq", y4)], writes=[cx.t("ssq", y4)])

                    def q5(j):
                        y4 = j % 4
                        act(ssq[y4][:], ssq[y4][:], AF.Ln, [cx.t("ssq", y4), cx.t("epsc")], [cx.t("ssq", y4)], scale=1.0 / 64.0, bias=epsc[:, 0:1])
                        act(ssq[y4][:], ssq[y4][:], AF.Exp, [cx.t("ssq", y4)], [cx.t("ssq", y4)], scale=-0.5)

                    def q6(j):
                        y4 = j % 4
                        s6 = j % 6
                        tt(ysq[y4][:].rearrange("p (h c) -> p h c", c=64), o2[s6][:].rearrange("p (h c) -> p h c", c=64),
                           ssq[y4][:].unsqueeze(2).to_broadcast([128, 8, 64]), ALU.mult, [cx.t("o2", s6), cx.t("ssq", y4)], [cx.t("ysq", y4)])

                    def q7(j):
                        s = j % 2
                        tt(mixt[s][:], ysq[j % 4][:], g2[j % 4][:], ALU.mult, [cx.t("ysq", j % 4), cx.t("g2", j % 4)], [cx.t("mixt", s)], e="pool")

                    def q8(j):
                        s = j % 2
                        tr, trt = bk(5 + s)
                        for q in range(4):
                            mm1(tr[:, q * 128:(q + 1) * 128], mixt[s][:, q * 128:(q + 1) * 128], identb[:], [cx.t("mixt", s), cx.t("identb")],
                                [trt], signal=(q == 3))

                    def q9(j):
                        s = j % 2
                        tb = j // 4
                        tr, trt = bk(5 + s)
                        trv = tr[:].rearrange("p (q t) -> p q t", t=128)
                        cp(hmT[:, 0:2, 2 + j * 128:2 + (j + 1) * 128], trv[:, 0:2, :], [trt], [cx.t("hm", 0, tb), cx.t("hm", 1, tb)], e="act")
                        cp(hmT[:, 6:8, 2 + j * 128:2 + (j + 1) * 128], trv[:, 2:4, :], [trt], [cx.t("hm", 6, tb), cx.t("hm", 7, tb)], e="act")

                    pipeline([q0, q1, q2, q3, q4, q5, q6, q7, q8, q9], list(range(NT)), loads=load2, lead=4)
                    cx.barrier()
                cx.barrier()
            wo = [w_next(8, 512, 2), w_next(8, 512, 1)]
            for tb in (3, 0, 1, 2):
                for m in range(8):
                    w, wt = wo[m // 4]
                    mm_ = m % 4
                    b_, bt_ = bk(m % 4)
                    mm_group(b_[:], [(w[:, k, mm_ * 128:(mm_ + 1) * 128], hmT[:, k, 2 + tb * 512:2 + (tb + 1) * 512]) for k in range(8)],
                             hm_trks(tb) + [wt], bt_)
                    tt(rT[:, m, tb * 512:(tb + 1) * 512], rT[:, m, tb * 512:(tb + 1) * 512], b_[:], ALU.add, [cx.t("rT", m, tb), bt_],
                       [cx.t("rT", m, tb)])
                if tb == 3:
                    act(sq2[:], rT[:, :, T - 2:T], AF.Square, rT_trks(3), [cx.t("sq2")])
                    b_, bt_ = bk(4)
                    for k in range(8):
                        mm1(b_[:, 0:2], onesb[:], sq2[:, k, :], [cx.t("sq2"), cx.t("onesb")], [bt_] if (k == 0 or k == 7) else (),
                            start=(k == 0), stop=(k == 7), signal=(k == 7))
                    act(rs2[:], b_[:, 0:2], AF.Ln, [bt_, cx.t("epsc")], [cx.t("rs2")], scale=1.0 / D, bias=epsc[:, 0:1])
                    act(rs2[:], rs2[:], AF.Exp, [cx.t("rs2")], [cx.t("rs2")], scale=-0.5)
                    x2v = xs2[:].rearrange("p (k c) -> p k c", c=2)
                    tt(x2v, rT[:, :, T - 2:T], gains[:, G_FFN:G_FFN + 8].unsqueeze(2).to_broadcast([128, 8, 2]), ALU.mult,
                       rT_trks(3) + [cx.t("gains")], [cx.t("xs2")])
                    tt(x2v, x2v, rs2[:].unsqueeze(1).to_broadcast([128, 8, 2]), ALU.mult, [cx.t("xs2"), cx.t("rs2")], [cx.t("xs2")])
                    cx.dma("sp", x2s[l][:, :], xs2[:], reads=[cx.t("xs2")], writes=[cx.t("x2s", l)])
                    cx.collective(x2s[l].ap().opt(), x2g[l].ap().opt(), GROUPS, reads=[cx.t("x2s", l)], writes=[cx.t("x2g", l)])
            if dump(l, 1):
                stopped = True
                break

            norm_to_hm(G_FFN)
            with ExitStack() as ph:
                xg2 = sb("xg2", [128, 4, 16], F32, ph)
                hh = sb("hh", [128, 16], F32, ph)
                cx.dma("sp", xg2[:], x2g[l].ap().rearrange("(r p) f -> p r f", p=128), reads=[cx.t("x2g", l)], writes=[cx.t("xg2")])
                ts(hh[:], xg2[:, 0, :], sel[:, 4:5], None, ALU.mult, None, [cx.t("xg2"), cx.t("sel")], [cx.t("hh")])
                for r in range(1, 4):
                    stt(hh[:], xg2[:, r, :], sel[:, 4 + r:5 + r], hh[:], ALU.mult, ALU.add, [cx.t("xg2"), cx.t("sel"), cx.t("hh")], [cx.t("hh")])
                cp(hmT[:, :, 0:2], hh[:].rearrange("p (k c) -> p k c", c=2), [cx.t("hh")], [cx.t("hm_halo")])

                actT = sb("actT", [128, 11, T], BF, ph)
                accs = [sb("accs%d" % i, [128, 512], F32, ph) for i in range(3)]
                tails = [sb("tail%d" % i, [128, 2], F32, ph) for i in range(3)]
                a1 = [sb("a1_%d" % i, [128, 512], BF, ph) for i in range(2)]
                cidx = 0
                for hf in range(2):
                    c0 = hf * 11
                    for cpi in range(6):
                        ncc = 2 if cpi < 5 else 1
                        w, wt = w_next(8, ncc * 256)
                        for ci in range(ncc):
                            c = c0 + 2 * cpi + ci
                            cl = c - c0
                            cw = lambda jj: convw[:, (l * NCH + c) * 3 + jj:(l * NCH + c) * 3 + jj + 1]
                            wg_ = lambda k: w[:, k, ci * 256:ci * 256 + 128]
                            wu_ = lambda k: w[:, k, ci * 256 + 128:ci * 256 + 256]
                            hb, hbt = bk(4)
                            mm_group(hb[:, 0:2], [(wg_(k), hmT[:, k, 0:2]) for k in range(8)], [cx.t("hm_halo"), wt], hbt)
                            ti = cidx % 3
                            cidx += 1
                            cp(tails[ti][:], hb[:, 0:2], [hbt], [cx.t("tail", ti)], e="act")
                            for tb in range(NB):
                                gb, gbt = bk(tb % 2)
                                mm_group(gb[:], [(wg_(k), hmT[:, k, 2 + tb * 512:2 + (tb + 1) * 512]) for k in range(8)], hm_trks(tb) + [wt], gbt)
                                ub, ubt = bk(2 + tb % 2)
                                mm_group(ub[:], [(wu_(k), hmT[:, k, 2 + tb * 512:2 + (tb + 1) * 512]) for k in range(8)], hm_trks(tb) + [wt], ubt)
                                ai = cidx % 3
                                acc, acct = accs[ai], cx.t("accs", ai)
                                act(acc[:], gb[:], AF.Identity, [gbt, cx.t("convw"), cx.t("convb")], [acct], scale=cw(2),
                                    bias=convb[:, l * NCH + c:l * NCH + c + 1])
                                stt(acc[:, 1:512], gb[:, 0:511], cw(1), acc[:, 1:512], ALU.mult, ALU.add, [gbt, acct, cx.t("convw")], [acct])
                                stt(acc[:, 2:512], gb[:, 0:510], cw(0), acc[:, 2:512], ALU.mult, ALU.add, [gbt, acct, cx.t("convw")], [acct])
                                tlt = cx.t("tail", ti)
                                stt(acc[:, 0:1], tails[ti][:, 1:2], cw(1), acc[:, 0:1], ALU.mult, ALU.add, [tlt, acct, cx.t("convw")], [acct])
                                stt(acc[:, 0:2], tails[ti][:, 0:2], cw(0), acc[:, 0:2], ALU.mult, ALU.add, [tlt, acct, cx.t("convw")], [acct])
                                if tb < NB - 1:
                                    ti = cidx % 3
                                    cidx += 1
                                    cp(tails[ti][:], gb[:, 510:512], [gbt], [cx.t("tail", ti)], e="act")
                                else:
                                    cidx += 1
                                sa = tb % 2
                                act(a1[sa][:], acc[:], AF.Gelu_apprx_tanh, [acct], [cx.t("a1", sa)])
                                tt(actT[:, cl, tb * 512:(tb + 1) * 512], a1[sa][:], ub[:], ALU.mult, [cx.t("a1", sa), ubt], [cx.t("actT", cl, tb)])
                    for mq in range(4):
                        w, wt = w_next(11, 256)
                        for mm_ in range(2):
                            m = mq * 2 + mm_
                            for tb in range(NB):
                                b_, bt_ = bk(6 + tb % 2)
                                mm_group(b_[:], [(w[:, cl, mm_ * 128:(mm_ + 1) * 128], actT[:, cl, tb * 512:(tb + 1) * 512]) for cl in range(11)],
                                         [cx.t("actT", cl, tb) for cl in range(11)] + [wt], bt_)
                                tt(rT[:, m, tb * 512:(tb + 1) * 512], rT[:, m, tb * 512:(tb + 1) * 512], b_[:], ALU.add, [cx.t("rT", m, tb), bt_],
                                   [cx.t("rT", m, tb)])
                cx.barrier()
            if dump(l, 2):
                stopped = True
                break

            with ExitStack() as ph:
                pT = sb("pT", [128, 2, T], BF, ph)
                pb = [sb("pb%d" % i, [128, 256], F32, ph) for i in range(2)]
                sg = [sb("sg%d" % i, [128, 512], F32, ph) for i in range(2)]
                for tb in range(NB):
                    pbk = [bk(0), bk(1)]
                    for jj in range(4):
                        j = tb * 4 + jj
                        s = j % 2
                        cx.dma("sp", pb[s][:], p_d[l, j * 128:(j + 1) * 128, :], writes=[cx.t("pb", s)])
                        for k2 in range(2):
                            mm1(pbk[k2][0][:, jj * 128:(jj + 1) * 128], pb[s][:, k2 * 128:(k2 + 1) * 128], identf, [cx.t("pb", s), cx.t("cst")],
                                [pbk[k2][1]], signal=True, tr=True)
                    for k2 in range(2):
                        cp(pT[:, k2, tb * 512:(tb + 1) * 512], pbk[k2][0][:], [pbk[k2][1]], [cx.t("pT", k2, tb)], e=("act" if k2 else "dve"))
                norm_to_hm(G_PLE)
                wq = [w_next(8, 512, 2), w_next(8, 512, 1)]
                wpp, wppt = w_next(2, 1024, 0)
                for u in range(2):
                    w, wt = wq[u]
                    for mm_ in range(4):
                        m = u * 4 + mm_
                        for tb in range(NB):
                            s = tb % 2
                            gb, gbt = bk(2 + s)
                            mm_group(gb[:], [(w[:, k, mm_ * 128:(mm_ + 1) * 128], hmT[:, k, 2 + tb * 512:2 + (tb + 1) * 512]) for k in range(8)],
                                     hm_trks(tb) + [wt], gbt)
                            ppb, ppbt = bk(4 + s)
                            mm_group(ppb[:], [(wpp[:, k2, m * 128:(m + 1) * 128], pT[:, k2, tb * 512:(tb + 1) * 512]) for k2 in range(2)],
                                     [cx.t("pT", 0, tb), cx.t("pT", 1, tb), wppt], ppbt)
                            act(sg[s][:], gb[:], AF.Sigmoid, [gbt], [cx.t("sg", s)])
                            tt(sg[s][:], sg[s][:], ppb[:], ALU.mult, [cx.t("sg", s), ppbt], [cx.t("sg", s)])
                            tt(rT[:, m, tb * 512:(tb + 1) * 512], rT[:, m, tb * 512:(tb + 1) * 512], sg[s][:], ALU.add,
                               [cx.t("rT", m, tb), cx.t("sg", s)], [cx.t("rT", m, tb)])
                cx.barrier()
            if dump(l, 3):
                stopped = True
                break

        if not stopped:
            with ExitStack() as ph:
                yT = sb("yT", [128, 8, 512], F32, ph)
                ot = [sb("ot%d" % i, [128, D], F32, ph) for i in range(2)]
                GF = DEPTH * 3 * 8
                for tb in range(NB):
                    rs, rst = rstd_block2(tb, 0)
                    for k in range(8):
                        stt(yT[:, k, :], rT[:, k, tb * 512:(tb + 1) * 512], gains[:, GF + k:GF + k + 1], rs[:], ALU.mult, ALU.mult,
                            [cx.t("rT", k, tb), rst, cx.t("gains")], [cx.t("yT", k)])
                    for jj in range(4):
                        j = tb * 4 + jj
                        s = j % 2
                        for half in range(2):
                            b_, bt_ = bk(1 + half + 2 * (j % 2))
                            for q in range(4):
                                m = half * 4 + q
                                mm1(b_[:, q * 128:(q + 1) * 128], yT[:, m, jj * 128:(jj + 1) * 128], identf, [cx.t("yT", m), cx.t("cst")], [bt_],
                                    signal=(q == 3), tr=True)
                            cp(ot[s][:, half * 512:(half + 1) * 512], b_[:], [bt_], [cx.t("ot", s)], e=("act" if half else "dve"))
                        cx.dma("sp", out_d[j * 128:(j + 1) * 128, :], ot[s][:], reads=[cx.t("ot", s)], writes=[cx.t("out")])
        cx.barrier(final=True)
        for i in range(cx.NDMA):
            n = "d%d" % i
            if cx.cnt[n] > 0:
                cx._wait("sp", [(n, cx.cnt[n])])
        if cx.cnt["cc"] > 0:
            cx._wait("sp", [("cc", cx.cnt["cc"])])
    return nc


def _swap_idx(base, ncols):
    j = np.arange(ncols)
    return base + (j // 64) * 64 + ((j % 64) + 32) % 64


def _in_cols():
    aq, af, ai, ag, bq, bk_, bv, cq, ck, cv, cg = 0, 256, 512, 768, 1024, 1536, 1664, 1792, 2048, 2304, 2560
    r = lambda b, n: b + np.arange(n)
    cols = []
    cols += [r(aq, 128), r(af, 128), r(aq + 128, 128), r(af + 128, 128)]
    for hp in range(2):
        cols += [r(cq + hp * 128, 128), r(ck + hp * 128, 128)]
    for c in range(4):
        cols += [np.concatenate([r(bq + c * 64, 64), r(bq + (4 + c) * 64, 64)])]
    cols += [r(bk_, 128)]
    cols += [r(ai, 256), r(cv, 256)]
    cols += [r(ag, 256), r(cg, 256)]
    cols += [r(bv, 128)]
    idx = np.concatenate(cols)
    assert idx.shape[0] == NIN
    return idx


def _consts():
    c = np.zeros((128, NCONST), np.float32)
    p = np.arange(128)
    c[:, C_ID:C_ID + 128] = np.eye(128, dtype=np.float32)
    t = np.arange(64)
    c[:, C_CAUS:C_CAUS + 64] = ((p[:, None] % 64) <= t[None, :]).astype(np.float32)
    q = np.arange(128)
    mb = np.zeros((128, 2, 128), np.float32)
    mb[:, 0, :] = np.where(p[:, None] > q[None, :], 0.0, NEG)
    mb[:, 1, :] = np.where(p[:, None] <= q[None, :], 0.0, NEG)
    c[:, C_MB:C_MB + 256] = mb.reshape(128, 256)
    seg = np.ones(512, np.float32)
    seg[::64] = 0.0
    c[:, C_SEG:C_SEG + 512] = seg[None, :]
    inv = 1.0 / (10000.0 ** (np.arange(0, 64, 2, dtype=np.float32) / np.float32(64)))
    inv = inv.astype(np.float32)
    c[:, C_INV] = inv[(p % 64) % 32]
    c[:, C_SIGN] = np.where((p % 64) < 32, -1.0, 1.0)
    tau = np.arange(64, dtype=np.float64)
    for hp in range(2):
        h = 2 * hp + p // 64
        lg = np.log(1.0 - 2.0 ** (-5.0 - h.astype(np.float64)))
        c[:, C_EAC + hp * 64:C_EAC + (hp + 1) * 64] = np.exp((tau[None, :] - 31.0) * lg[:, None])
        c[:, C_EBC + hp * 64:C_EBC + (hp + 1) * 64] = np.exp((31.0 - tau[None, :]) * lg[:, None]) * 0.125
        c[:, C_LNC + hp] = 32.0 * lg
        c[:, C_EC + hp] = np.exp(32.0 * lg)
    sig = (p // 64) * 64 + ((p % 64) + 32) % 64
    pm = np.zeros((128, 128), np.float32)
    pm[sig, p] = 1.0
    c[:, C_PM:C_PM + 128] = pm
    return c


def _fm(w):
    K, N = w.shape
    return np.ascontiguousarray(w.reshape(K // 128, 128, N).transpose(1, 0, 2))


def _prep_inputs(x, p, positions, attn_norm, w_in, hgrn_lb, hgrn_gnorm, attn_sinks, w_out, ffn_norm, w_gate, w_up, conv_w,
                 conv_b, w_down, ple_norm, w_ple_gate, w_ple_proj, final_norm):
    f32 = lambda a: np.ascontiguousarray(np.asarray(a, dtype=np.float32))
    x, p = f32(x), f32(p)
    positions = np.asarray(positions).astype(np.int32)
    idx = _in_cols()
    w_in_r = np.stack([_fm(f32(w_in[l])[:, idx]) for l in range(DEPTH)])
    w_out_r = np.stack([_fm(f32(w_out[l])) for l in range(DEPTH)])
    wgu = []
    for l in range(DEPTH):
        g = _fm(f32(w_gate[l])).reshape(128, 8, NCH, 1, 128)
        u = _fm(f32(w_up[l])).reshape(128, 8, NCH, 1, 128)
        wgu.append(np.concatenate([g, u], axis=3).reshape(128, 8, 2 * DFF))
    w_gu_r = np.stack(wgu)
    w_dn_r = np.stack([_fm(f32(w_down[l])) for l in range(DEPTH)])
    w_pg_r = np.stack([_fm(f32(w_ple_gate[l])) for l in range(DEPTH)])
    w_pp_r = np.stack([_fm(f32(w_ple_proj[l])) for l in range(DEPTH)])
    gl = []
    for l in range(DEPTH):
        for a in (attn_norm, ffn_norm, ple_norm):
            gl.append(f32(a[l]).reshape(8, 128).T)
    gl.append(f32(final_norm).reshape(8, 128).T)
    gains = np.ascontiguousarray(np.concatenate(gl, axis=1))
    lb = np.ascontiguousarray(f32(hgrn_lb).reshape(DEPTH, 2, 128).transpose(2, 1, 0).reshape(128, 2 * DEPTH))
    gn = f32(hgrn_gnorm).reshape(1, DEPTH * 256)
    sinks = f32(attn_sinks).reshape(1, DEPTH * 8)
    cw = np.ascontiguousarray(f32(conv_w).reshape(DEPTH, 3, NCH, 128).transpose(3, 0, 2, 1).reshape(128, DEPTH * NCH * 3))
    cb = np.ascontiguousarray(f32(conv_b).reshape(DEPTH, NCH, 128).transpose(2, 0, 1).reshape(128, DEPTH * NCH))
    consts = _consts()
    shared = dict(consts=consts, w_in=w_in_r, w_out=w_out_r, w_gu=w_gu_r, w_dn=w_dn_r, w_pg=w_pg_r, w_pp=w_pp_r, gains=gains, lb=lb,
                  gnorm=gn, sinks=sinks, convw=cw, convb=cb)
    in_maps = []
    for c in range(NCORES):
        b, rho = c // 4, c % 4
        sl = slice(rho * T, (rho + 1) * T)
        sel = np.zeros((128, 8), np.float32)
        sel[:, rho] = 1.0
        if rho > 0:
            sel[:, 4 + rho - 1] = 1.0
        m = dict(shared)
        m["x"] = np.ascontiguousarray(x[b, sl, :])
        m["p"] = np.ascontiguousarray(p[:, b, sl, :])
        m["pos"] = np.ascontiguousarray(positions[b, sl].reshape(1, T))
        m["sel"] = sel
        in_maps.append(m)
    return in_maps


_NC_CACHE = {}


def kernel(x, p, positions, attn_norm, w_in, hgrn_lb, hgrn_gnorm, attn_sinks, w_out, ffn_norm, w_gate, w_up, conv_w, conv_b,
           w_down, ple_norm, w_ple_gate, w_ple_proj, final_norm):
    in_maps = _prep_inputs(x, p, positions, attn_norm, w_in, hgrn_lb, hgrn_gnorm, attn_sinks, w_out, ffn_norm, w_gate, w_up,
                           conv_w, conv_b, w_down, ple_norm, w_ple_gate, w_ple_proj, final_norm)
    nc = build_program()
    res = run_bass_kernel_spmd(nc, in_maps, core_ids=list(range(NCORES)))
    out = np.zeros((2, 4 * T, D), np.float32)
    for c in range(NCORES):
        b, rho = c // 4, c % 4
        out[b, rho * T:(rho + 1) * T, :] = res.results[c]["out"]
    return out
```
